# Optimizing a Trainium2 kernel written in Bass

```python
import math
import jax
import jax.numpy as jnp
from jax import lax
import numpy as np

D_MODEL = 1024
BATCH = 8
SEQ = 4096
DEPTH = 2

F32 = jnp.float32
EPS = 1e-6
TINY = 1e-30
N_BRANCH = 4
D_BRANCH = D_MODEL // 4

A_HEADS = 4
A_DK = D_BRANCH // A_HEADS
A_DV = D_BRANCH // A_HEADS
A_CHUNK = 64
D_B = D_BRANCH
B_BLOCKS = 4
B_BW = D_B // B_BLOCKS
B_CONV = 4
LRU_C = 8.0
D_C = D_BRANCH
C_ORDER = 2
C_CONV = 3
C_EMB = 33
C_HID = 64
C_MIN_DECAY = math.log(1e-2) / 1.5
C_MAX_DECAY = math.log(1e-2) / 0.3
D_GROUPS = ((128, 1), (512, 4), (2048, 16))
D_HEADS_PER_GROUP = 4
D_HEAD_DIM = D_BRANCH // D_HEADS_PER_GROUP
D_N_HEADS = len(D_GROUPS) * D_HEADS_PER_GROUP
D_QKV = D_N_HEADS * D_HEAD_DIM
N_BUCKETS = 32
MAX_DISTANCE = 1024
NEG_BIG = -1e30
D_FF = -(-8 * D_MODEL // (3 * 256)) * 256
IN_A = 5 * D_BRANCH
IN_B = 2 * D_B
IN_C = 3 * D_C
IN_D = 3 * D_QKV
IN_WIDTH = IN_A + IN_B + IN_C + IN_D
MIXER_OFFSETS = (IN_A, IN_A + IN_B, IN_A + IN_B + IN_C)

kernel_name = "hybrid_gated_hgrn2_rglru_hyena_dilated_encoder"


def rmsnorm(x, g):
    xf = x.astype(F32)
    y = xf * lax.rsqrt(jnp.mean(xf * xf, axis=-1, keepdims=True) + EPS)
    return (y * g.astype(F32)).astype(x.dtype)


def dwconv(x, w, b, left):
    K, C = w.shape
    y = lax.conv_general_dilated(x, w[:, None, :].astype(x.dtype), window_strides=(1,),
                                 padding=[(left, K - 1 - left)],
                                 dimension_numbers=("NWC", "WIO", "NWC"),
                                 feature_group_count=C)
    return y + b.astype(x.dtype)


def hgrn2_bidir(q, f_logit_fwd, f_logit_bwd, v, lb):
    B, S, H, DK = q.shape
    DV = v.shape[-1]
    C = A_CHUNK
    nc = S // C
    lb = lb.astype(F32)

    def forget(fl):
        fl = fl.astype(F32)
        f = lb + (1.0 - lb) * jax.nn.sigmoid(fl)
        log_f = jnp.log(jnp.maximum(f, TINY))
        return log_f, (1.0 - lb) * jax.nn.sigmoid(-fl)

    lf_fwd, k_fwd = forget(f_logit_fwd)
    lf_bwd, k_bwd = forget(f_logit_bwd)
    rev = lambda t: t[:, ::-1]
    qf, vf = q.astype(F32), v.astype(F32)

    def to_chunks(a_fwd, a_bwd):
        t = jnp.stack([a_fwd, rev(a_bwd)])
        return t.reshape(2, B, nc, C, H, t.shape[-1]).transpose(2, 0, 1, 4, 3, 5)

    qs, ks = to_chunks(qf, qf), to_chunks(k_fwd, k_bwd)
    gs, vs = to_chunks(lf_fwd, lf_bwd), to_chunks(vf, vf)
    tril = jnp.tril(jnp.ones((C, C), bool))[:, :, None]

    def step(state, inp):
        qc, kc, gc, vc = inp
        b = jnp.cumsum(gc, axis=-2)
        diff = b[..., :, None, :] - b[..., None, :, :]
        decay = jnp.where(tril, jnp.exp(jnp.where(tril, diff, 0.0)), 0.0)
        scores = jnp.einsum('zbhtk,zbhtsk,zbhsk->zbhts', qc, decay, kc)
        o = (jnp.einsum('zbhts,zbhsv->zbhtv', scores, vc)
             + jnp.einsum('zbhtk,zbhkv->zbhtv', qc * jnp.exp(b), state))
        b_last = b[..., -1:, :]
        state = (state * jnp.exp(b_last)[..., 0, :, None]
                 + jnp.einsum('zbhsk,zbhsv->zbhkv', kc * jnp.exp(b_last - b), vc))
        return state, o

    s0 = jnp.zeros((2, B, H, DK, DV), F32)
    _, o = lax.scan(step, s0, (qs, ks, gs, vs))
    o = o.transpose(1, 2, 0, 4, 3, 5).reshape(2, B, S, H, DV)
    return o[0] + rev(o[1])


def lin_scan(a, u, reverse):
    def comb(e1, e2):
        a1, b1 = e1
        a2, b2 = e2
        return a1 * a2, a2 * b1 + b2
    _, h = lax.associative_scan(comb, (a, u), axis=1, reverse=reverse)
    return h


def rglru_bidir(x, wa, ba, wx, bx, lam):
    B, S, Dr = x.shape
    xf = x.astype(F32)
    xb = xf.reshape(B, S, B_BLOCKS, B_BW)

    def blockdiag(w, b):
        y = jnp.einsum('bsnc,zncd->zbsnd', xb, w.astype(F32)).reshape(2, B, S, Dr)
        return y + b.astype(F32)[:, None, None]

    r = jax.nn.sigmoid(blockdiag(wa, ba))
    ig = jax.nn.sigmoid(blockdiag(wx, bx))
    log_a = -LRU_C * r * jax.nn.softplus(-lam.astype(F32))[:, None, None]
    a = jnp.exp(log_a)
    u = jnp.sqrt(jnp.maximum(-jnp.expm1(2.0 * log_a), 0.0)) * ig * xf[None]
    return lin_scan(a[0], u[0], False) + lin_scan(a[1], u[1], True)


def hyena_filters(L, w1, b1, freq, w2, b2, w3):
    t = jnp.linspace(0.0, 1.0, L, dtype=F32)[:, None]
    bands = (C_EMB - 1) // 2
    w = 2.0 * math.pi * jnp.arange(L, dtype=F32)[:, None] / L
    fr = jnp.linspace(1e-4, bands - 1, bands, dtype=F32)[None]
    z = jnp.concatenate([t, jnp.cos(fr * w), -jnp.sin(fr * w)], axis=-1)
    freq = freq.astype(F32)
    hdn = jnp.sin(freq * (z @ w1.astype(F32) + b1.astype(F32)))
    hdn = jnp.sin(freq * (hdn @ w2.astype(F32) + b2.astype(F32)))
    hf = (hdn @ w3.astype(F32)).reshape(L, C_ORDER, 2, D_C)
    deltas = jnp.linspace(C_MIN_DECAY, C_MAX_DECAY, D_C, dtype=F32)
    hf = hf * jnp.exp(-t * jnp.abs(deltas))[:, None, None, :]
    return hf * lax.rsqrt(jnp.sum(hf * hf, axis=0, keepdims=True) + EPS)


def bidir_fftconv(u, h_fwd, h_bwd, bias):
    L = u.shape[1]
    n = 2 * L
    uf = u.astype(F32)
    U = jnp.fft.rfft(uf, n=n, axis=1)
    H = jnp.fft.rfft(h_fwd, n=n, axis=0) + jnp.conj(jnp.fft.rfft(h_bwd, n=n, axis=0))
    y = jnp.fft.irfft(U * H[None], n=n, axis=1)[:, :L]
    return y + uf * bias.astype(F32)


def t5_bucket(rel):
    half = N_BUCKETS // 2
    max_exact = half // 2
    n = np.abs(rel)
    large = max_exact + (np.log(np.maximum(n, 1) / max_exact) / math.log(MAX_DISTANCE / max_exact)
                         * (half - max_exact)).astype(np.int64)
    large = np.minimum(large, half - 1)
    return (rel > 0).astype(np.int64) * half + np.where(n < max_exact, n, large)


def banded_attention(q, k, v, bias_band, half):
    N, H, n, dh = q.shape
    W = half
    nb = -(-n // W)
    n_pad = nb * W
    qp = jnp.pad(q, ((0, 0), (0, 0), (0, n_pad - n), (0, 0))).reshape(N, H, nb, W, dh)

    def kblocks(t):
        tp = jnp.pad(t, ((0, 0), (0, 0), (W, n_pad - n + W), (0, 0))).reshape(N, H, nb + 2, W, dh)
        return jnp.concatenate([tp[:, :, :-2], tp[:, :, 1:-1], tp[:, :, 2:]], axis=3)

    kb, vb = kblocks(k), kblocks(v)
    a_idx = np.arange(W)[:, None]
    c_idx = np.arange(3 * W)[None, :]
    rel = c_idx - W - a_idx
    key_pos = (np.arange(nb)[:, None, None] - 1) * W + c_idx[None]
    valid = (np.abs(rel) <= W)[None] & (key_pos >= 0) & (key_pos < n)
    bias = bias_band[:, np.clip(rel + W, 0, 2 * W)]
    s = jnp.einsum('zhbqd,zhbkd->zhbqk', qp, kb) * (dh ** -0.5) + bias[:, None]
    s = jnp.where(valid[None, None], s, NEG_BIG)
    m = jnp.max(s, axis=-1, keepdims=True)
    p = jnp.exp(s - m)
    l = jnp.sum(p, axis=-1)
    o = jnp.einsum('zhbqk,zhbkd->zhbqd', p, vb) / l[..., None]
    lse = m[..., 0] + jnp.log(l)
    o = o.reshape(N, H, n_pad, dh)[:, :, :n]
    lse = lse.reshape(N, H, n_pad)[:, :, :n]
    return o, lse


def dilated_attention(q, k, v, rel_bias):
    B, S, _, dh = q.shape
    G = D_HEADS_PER_GROUP
    rel_bias = rel_bias.astype(F32)
    outs, lses = [], []
    for g, (win, dil) in enumerate(D_GROUPS):
        hs = slice(g * G, (g + 1) * G)
        n = S // dil

        def gather(t):
            t = t[:, :, hs].astype(F32).reshape(B, n, dil, G, dh)
            return t.transpose(0, 2, 3, 1, 4).reshape(B * dil, G, n, dh)

        half = win // (2 * dil)
        offsets = np.arange(-half, half + 1) * dil
        bias_band = rel_bias[t5_bucket(offsets)][:, hs].T
        o, lse = banded_attention(gather(q), gather(k), gather(v), bias_band, half)
        outs.append(o.reshape(B, dil, G, n, dh).transpose(0, 3, 1, 2, 4).reshape(B, S, G, dh))
        lses.append(lse.reshape(B, dil, G, n).transpose(0, 3, 1, 2).reshape(B, S, G))
    wts = jax.nn.softmax(jnp.stack(lses), axis=0)
    return jnp.sum(wts[..., None] * jnp.stack(outs), axis=0)


def setup_inputs(seed: int = 0) -> dict:
    key = jax.random.key(seed)
    ks = iter(jax.random.split(key, 40))
    L = DEPTH

    def nrm(shape, scale):
        return scale * jax.random.normal(next(ks), shape, F32)

    lam_u = jax.random.uniform(next(ks), (L, 2, D_B), F32, 0.9, 0.999)
    s = lam_u ** (1.0 / LRU_C)
    lru_lambda = jnp.log(s) - jnp.log1p(-s)
    return {
        "x": nrm((BATCH, SEQ, D_MODEL), 1.0),
        "norm1_g": 1.0 + nrm((L, D_MODEL), 0.01),
        "w_in": nrm((L, D_MODEL, IN_WIDTH), D_MODEL ** -0.5),
        "hgrn_lb_logits": nrm((L, D_BRANCH), 0.5),
        "hgrn_norm_g": 1.0 + nrm((L, D_BRANCH), 0.01),
        "lru_conv_w": nrm((L, B_CONV, D_B), B_CONV ** -0.5),
        "lru_conv_b": nrm((L, D_B), 0.02),
        "lru_wa": nrm((L, 2, B_BLOCKS, B_BW, B_BW), B_BW ** -0.5),
        "lru_ba": nrm((L, 2, D_B), 0.1),
        "lru_wx": nrm((L, 2, B_BLOCKS, B_BW, B_BW), B_BW ** -0.5),
        "lru_bx": nrm((L, 2, D_B), 0.1),
        "lru_lambda": lru_lambda,
        "hy_conv_w": nrm((L, C_CONV, 3 * D_C), C_CONV ** -0.5),
        "hy_conv_b": nrm((L, 3 * D_C), 0.02),
        "hy_w1": nrm((L, C_EMB, C_HID), C_EMB ** -0.5),
        "hy_b1": nrm((L, C_HID), 0.1),
        "hy_freq": 1.0 + nrm((L, C_HID), 0.1),
        "hy_w2": nrm((L, C_HID, C_HID), C_HID ** -0.5),
        "hy_b2": nrm((L, C_HID), 0.1),
        "hy_w3": nrm((L, C_HID, C_ORDER * 2 * D_C), C_HID ** -0.5),
        "hy_bias": nrm((L, C_ORDER, D_C), 0.1),
        "rel_bias": nrm((N_BUCKETS, D_N_HEADS), 0.2),
        "w_branch": nrm((L, N_BRANCH, D_BRANCH, D_MODEL), D_BRANCH ** -0.5),
        "w_gate": nrm((L, D_MODEL, N_BRANCH, D_MODEL), D_MODEL ** -0.5),
        "b_gate": nrm((L, N_BRANCH, D_MODEL), 0.1),
        "w_out": nrm((L, D_MODEL, D_MODEL), D_MODEL ** -0.5),
        "norm2_g": 1.0 + nrm((L, D_MODEL), 0.01),
        "w_ff1": nrm((L, D_MODEL, D_FF), D_MODEL ** -0.5),
        "w_ff3": nrm((L, D_MODEL, D_FF), D_MODEL ** -0.5),
        "w_ff2": nrm((L, D_FF, D_MODEL), D_FF ** -0.5),
        "final_g": 1.0 + nrm((D_MODEL,), 0.01),
    }


def reference(x, norm1_g, w_in, hgrn_lb_logits, hgrn_norm_g, lru_conv_w, lru_conv_b, lru_wa, lru_ba,
              lru_wx, lru_bx, lru_lambda, hy_conv_w, hy_conv_b, hy_w1, hy_b1, hy_freq, hy_w2, hy_b2,
              hy_w3, hy_bias, rel_bias, w_branch, w_gate, b_gate, w_out, norm2_g, w_ff1, w_ff3, w_ff2,
              final_g):
    B, S, _ = x.shape
    lb_soft = jax.nn.softmax(hgrn_lb_logits.astype(F32), axis=0)
    lower_bounds = jnp.cumsum(lb_soft, axis=0) - lb_soft[0]
    for l in range(DEPTH):
        h = rmsnorm(x, norm1_g[l])
        proj = h @ w_in[l]
        pA, pB, pC, pD = jnp.split(proj, MIXER_OFFSETS, axis=-1)

        qA, fA_fwd, fA_bwd, iA, gA = jnp.split(pA, 5, axis=-1)
        heads = lambda t: t.reshape(B, S, A_HEADS, -1)
        oA = hgrn2_bidir(heads(qA), heads(fA_fwd), heads(fA_bwd), heads(iA),
                         lower_bounds[l].reshape(A_HEADS, A_DK))
        oA = rmsnorm(oA, hgrn_norm_g[l].reshape(A_HEADS, A_DV)).reshape(B, S, D_BRANCH)
        yA = (oA * jax.nn.silu(gA.astype(F32))).astype(x.dtype)

        xB, gB = jnp.split(pB, 2, axis=-1)
        xB = dwconv(xB, lru_conv_w[l], lru_conv_b[l], B_CONV // 2)
        hB = rglru_bidir(xB, lru_wa[l], lru_ba[l], lru_wx[l], lru_bx[l], lru_lambda[l])
        yB = (hB * jax.nn.gelu(gB.astype(F32))).astype(x.dtype)

        uC = dwconv(pC, hy_conv_w[l], hy_conv_b[l], C_CONV // 2)
        vC, x1, x2 = jnp.split(uC, 3, axis=-1)
        filt = hyena_filters(S, hy_w1[l], hy_b1[l], hy_freq[l], hy_w2[l], hy_b2[l], hy_w3[l])
        z = x1.astype(F32) * bidir_fftconv(vC, filt[:, 0, 0], filt[:, 0, 1], hy_bias[l, 0])
        yC = (x2.astype(F32) * bidir_fftconv(z, filt[:, 1, 0], filt[:, 1, 1], hy_bias[l, 1])).astype(x.dtype)

        qD, kD, vD = [t.reshape(B, S, D_N_HEADS, D_HEAD_DIM) for t in jnp.split(pD, 3, axis=-1)]
        yD = dilated_attention(qD, kD, vD, rel_bias).reshape(B, S, D_BRANCH).astype(x.dtype)

        mixed = jnp.zeros_like(x)
        for j, y in enumerate((yA, yB, yC, yD)):
            gate = jax.nn.sigmoid(h @ w_gate[l, :, j] + b_gate[l, j])
            mixed = mixed + gate * (y @ w_branch[l, j])
        x = x + mixed @ w_out[l]

        h2 = rmsnorm(x, norm2_g[l])
        x = x + (jax.nn.silu(h2 @ w_ff1[l]) * (h2 @ w_ff3[l])) @ w_ff2[l]
    return rmsnorm(x, final_g)
```

```python
import math
import numpy as np
import ml_dtypes
from contextlib import ExitStack
import concourse.bass as bass
import concourse.mybir as mybir
from concourse.bass_utils import run_bass_kernel_spmd

F32 = mybir.dt.float32
BF16 = mybir.dt.bfloat16
AF = mybir.ActivationFunctionType
ALU = mybir.AluOpType

S = 4096
D = 1024
NT = S // 128
DEPTH = 2
DFF = 2816
NFF = DFF // 128
INW = 4864
PAD = 8
EPS = 1e-6
NDS = 40
NHW = 24


class Prog:
    def __init__(self, nc, es):
        self.nc = nc
        self.es = es
        self.eng = {'pe': nc.tensor, 'act': nc.scalar, 'dve': nc.vector, 'pool': nc.gpsimd, 'sp': nc.sync}
        self.streams = {k: [] for k in self.eng}
        self.sem = {k: es.enter_context(nc.semaphore("sem_" + k)) for k in ['pe', 'act', 'dve', 'pool']}
        self.cnt = {k: 0 for k in self.sem}
        self.dsem = [es.enter_context(nc.semaphore("dsem%d" % i)) for i in range(NDS)]
        self.dcnt = [0] * NDS
        self.dnext = 0
        self.dnext_sw = 0
        self.known = {k: {} for k in self.eng}
        self.res = {}
        self.nops = 0

    def sb(self, name, shape, dt):
        return self.es.enter_context(self.nc.sbuf_tensor(name, list(shape), dt))

    def ps(self, name, shape, dt):
        return self.es.enter_context(self.nc.psum_tensor(name, list(shape), dt))

    def _semh(self, sk):
        return self.sem[sk] if isinstance(sk, str) else self.dsem[sk[1]]

    def _wait(self, eng, ev):
        if ev is None:
            return
        sk, val = ev
        if sk == 'pe' and eng == 'pe':
            return
        if self.known[eng].get(sk, 0) >= val:
            return
        self.known[eng][sk] = val
        self.streams[eng].append(('w', self._semh(sk), val))

    def _deps(self, eng, r, w):
        for k in r:
            st = self.res.get(k)
            if st is not None:
                self._wait(eng, st[0])
        for k in w:
            st = self.res.get(k)
            if st is not None:
                self._wait(eng, st[0])
                for ev in st[1].items():
                    self._wait(eng, ev)

    def _commit(self, ev, r, w):
        for k in r:
            st = self.res.setdefault(k, [None, {}])
            st[1][ev[0]] = ev[1]
        for k in w:
            self.res[k] = [ev, {}]

    @staticmethod
    def _is_psum(k):
        return k == "psb" or (isinstance(k, tuple) and k[0] == "psf")

    def op(self, eng, fn, r=(), w=(), serial=False):
        if serial and eng == 'pe' and self.cnt['pe']:
            val = self.cnt['pe']
            if self.known['pe'].get('pe', 0) < val:
                self.known['pe']['pe'] = val
                self.streams['pe'].append(('w', self.sem['pe'], val))
        if eng != 'pe':
            px = [k for k in r if self._is_psum(k)]
            if px:
                r = [k for k in r if not self._is_psum(k)]
                w = list(w) + px
        self._deps(eng, r, w)
        self.cnt[eng] += 1
        ev = (eng, self.cnt[eng])
        self.streams[eng].append(('o', fn, self.sem[eng], 1))
        self._commit(ev, r, w)
        self.nops += 1

    def dma(self, q, out, in_, r=(), w=(), slow=False):
        if q == 'pool':
            i = NHW + self.dnext_sw
            self.dnext_sw = (self.dnext_sw + 1) % (NDS - NHW)
        else:
            i = self.dnext
            self.dnext = (self.dnext + 1) % NHW
        sk = ('d', i)
        self._wait(q, (sk, self.dcnt[i] * 16))
        self._deps(q, r, w)
        self.dcnt[i] += 1
        ev = (sk, self.dcnt[i] * 16)
        self.streams[q].append(('o', lambda e, o=out, n=in_: e.dma_start(out=o, in_=n, allow_slow_non_contiguous=slow), self.dsem[i], 16))
        self._commit(ev, r, w)
        self.nops += 1
        return ev

    def uniq(self, base):
        self._u = getattr(self, "_u", 0) + 1
        return "%s_%d" % (base, self._u)

    def barrier(self):
        for e in self.eng:
            self.wait_all(e)

    def wait_all(self, eng):
        for k in list(self.sem):
            if self.cnt[k]:
                self._wait(eng, (k, self.cnt[k]))
        for i in range(NDS):
            if self.dcnt[i]:
                self._wait(eng, (('d', i), self.dcnt[i] * 16))

    def emit(self):
        nc = self.nc
        streams = self.streams
        self.streams = {k: [] for k in self.eng}
        with nc.Block() as block:
            def replay(e, items):
                for it in items:
                    if it[0] == 'w':
                        e.wait_ge(it[1], it[2])
                    else:
                        it[1](e).then_inc(it[2], it[3])

            @block.sync
            def _(e):
                replay(e, streams['sp'])

            @block.tensor
            def _(e):
                replay(e, streams['pe'])

            @block.scalar
            def _(e):
                replay(e, streams['act'])

            @block.vector
            def _(e):
                replay(e, streams['dve'])

            @block.gpsimd
            def _(e):
                replay(e, streams['pool'])


class Rot:
    def __init__(self, name, n):
        self.name = name
        self.n = n
        self.i = 0

    def next(self):
        j = self.i
        self.i = (self.i + 1) % self.n
        return j, (self.name, j)


class K:
    def __init__(self, nc, es, dbg):
        self.nc = nc
        self.P = Prog(nc, es)
        self.dbg = dbg
        self.T = {}

    def din(self, name, shape, dt=F32):
        self.T[name] = self.nc.dram_tensor(name, list(shape), dt, kind="ExternalInput").ap()
        return self.T[name]

    def dout(self, name, shape, dt=F32):
        self.T[name] = self.nc.dram_tensor(name, list(shape), dt, kind="ExternalOutput").ap()
        return self.T[name]

    def dint(self, name, shape, dt=F32):
        self.T[name] = self.nc.dram_tensor(name, list(shape), dt, kind="Internal").ap()
        return self.T[name]


WEIGHT_SPECS = [
    ("norm1_g", (DEPTH, D)), ("w_in", (DEPTH, D, INW)), ("hgrn_lb_logits", (DEPTH, 256)),
    ("hgrn_norm_g", (DEPTH, 256)), ("lru_conv_w", (DEPTH, 4, 256)), ("lru_conv_b", (DEPTH, 256)),
    ("lru_wa", (DEPTH, 2, 4, 64, 64)), ("lru_ba", (DEPTH, 2, 256)), ("lru_wx", (DEPTH, 2, 4, 64, 64)),
    ("lru_bx", (DEPTH, 2, 256)), ("lru_lambda", (DEPTH, 2, 256)), ("hy_conv_w", (DEPTH, 3, 768)),
    ("hy_conv_b", (DEPTH, 768)), ("hy_w1", (DEPTH, 33, 64)), ("hy_b1", (DEPTH, 64)), ("hy_freq", (DEPTH, 64)),
    ("hy_w2", (DEPTH, 64, 64)), ("hy_b2", (DEPTH, 64)), ("hy_w3", (DEPTH, 64, 1024)), ("hy_bias", (DEPTH, 2, 256)),
    ("rel_bias", (32, 12)), ("w_branch", (DEPTH, 4, 256, D)), ("w_gate", (DEPTH, D, 4, D)),
    ("b_gate", (DEPTH, 4, D)), ("w_out", (DEPTH, D, D)), ("norm2_g", (DEPTH, D)), ("w_ff1", (DEPTH, D, DFF)),
    ("w_ff3", (DEPTH, D, DFF)), ("w_ff2", (DEPTH, DFF, D)), ("final_g", (D,)),
]


def build(cfg):
    nc = bass.Bass("TRN2", target_bir_lowering=False)
    es = ExitStack()
    k = K(nc, es, cfg)
    P = k.P
    T = k.T
    k.din("x", (S, D))
    for name, shp in WEIGHT_SPECS:
        k.din(name, shp)
    for name, (shp, dt) in CONST_SPECS.items():
        k.din(name, shp, dt)
    k.dout("out", (S, D))
    k.dint("xres", (S, D))
    k.dint("Og", (3, S, 260))
    k.dint("bandD", (3, 4, 512))
    k.dint("Hd", (33, 128, 2, 2, 256))
    layers = cfg.get("layers", list(range(DEPTH)))
    for l in layers:
        if cfg.get("y_in"):
            k.din("yT%d" % l, (4, 2, 128, S), BF16)
        elif cfg.get("y_out"):
            k.dout("yT%d" % l, (4, 2, 128, S), BF16)
        else:
            k.dint("yT%d" % l, (4, 2, 128, S), BF16)

    hT = P.sb("hT", (128, 8, S + 2 * PAD), BF16)
    ident = P.sb("ident", (128, 128), BF16)
    gb = P.sb("gb", (128, D), F32)
    stat = P.sb("stat", (128, 8), F32)
    epsc = P.sb("epsc", (128, 1), F32)
    psf = [P.ps("psf%d" % i, (128, 512), F32) for i in range(7)]
    psb = P.ps("psb", (128, 8, 128), BF16)
    psr = Rot("psf", 7)
    k.hT, k.ident, k.psf, k.psb, k.psr, k.gb, k.stat, k.epsc = hT, ident, psf, psb, psr, gb, stat, epsc

    P.dma('sp', ident[:], T["c_ident"], w=["ident"])
    P.op('dve', lambda e: e.memset(epsc[:], EPS), w=["epsc"])
    P.op('dve', lambda e: e.memset(hT[:, :, 0:PAD], 0.0), w=[("hT", 0)])
    P.op('dve', lambda e: e.memset(hT[:, :, PAD + S:PAD + S + PAD], 0.0), w=[("hT", NT - 1)])

    def load_gain(src_row):
        P.dma('sp', gb[:], src_row.partition_broadcast(128), w=["gb"])

    class NT_bufs:
        def __init__(self, st):
            self.xb = [P.sb(P.uniq("xb"), (128, D), F32) for i in range(2)]
            self.hb = [P.sb(P.uniq("hb"), (128, D), BF16) for i in range(2)]
            self.junk = P.sb(P.uniq("junk"), (128, D), BF16)
            self.rx = Rot(P.uniq("xbk"), 2)
            self.rhb = Rot(P.uniq("hbk"), 2)

    def norm_transpose(nb, xt, xkey, i):
        junk = nb.junk
        P.op('act', lambda e: e.activation(out=junk[:], in_=xt, func=AF.Square, accum_out=stat[:, 0:1]),
             r=[xkey], w=["junk", "stat"])
        P.op('act', lambda e: e.activation(out=stat[:, 1:2], in_=stat[:, 0:1], func=AF.Ln, bias=epsc[:, 0:1],
                                           scale=1.0 / D), r=["stat", "epsc"], w=["stat"])
        P.op('act', lambda e: e.activation(out=stat[:, 2:3], in_=stat[:, 1:2], func=AF.Exp, scale=-0.5),
             r=["stat"], w=["stat"])
        j, hk = nb.rhb.next()
        hbt = nb.hb[j]
        P.op('dve', lambda e: e.scalar_tensor_tensor(out=hbt[:], in0=xt, scalar=stat[:, 2:3], in1=gb[:],
                                                     op0=ALU.mult, op1=ALU.mult), r=[xkey, "stat", "gb"], w=[hk])
        for c in range(8):
            P.op('pe', lambda e, c=c: e.transpose(out=psb[:, c, :], in_=hbt[:, c * 128:(c + 1) * 128],
                                                  identity=ident[:]), r=[hk, "ident"], w=["psb"])
        P.op('act', lambda e: e.copy(out=hT[:, :, PAD + i * 128:PAD + (i + 1) * 128], in_=psb[:]),
             r=["psb"], w=[("hT", i)])

    def stage_norm1(l):
        with ExitStack() as st:
            P.es = st
            nb = NT_bufs(st)
            src = T["x"] if l == 0 else T["xres"]
            load_gain(T["norm1_g"][l])
            for i in range(NT):
                j, xk = nb.rx.next()
                P.dma('sp', nb.xb[j][:], src[i * 128:(i + 1) * 128, :], r=[("xres", i)], w=[xk])
                norm_transpose(nb, nb.xb[j][:], xk, i)
            P.barrier()
            P.emit()

    def stage_merge(l, mixedT):
        with ExitStack() as st:
            P.es = st
            wblk = [P.sb(P.uniq("wblk"), (128, 8, 4, 128), BF16) for i in range(2)]
            wbb = [P.sb(P.uniq("wbb"), (128, 4, 2, 128), BF16) for i in range(2)]
            ych = [P.sb(P.uniq("ych"), (128, 4, 2, 512), BF16) for i in range(2)]
            bgT = P.sb(P.uniq("bgT"), (128, 32), F32)
            sig = [P.sb(P.uniq("sig"), (128, 512), F32) for i in range(2)]
            tmp = [P.sb(P.uniq("tmp"), (128, 512), F32) for i in range(2)]
            acc = [P.sb(P.uniq("acc"), (128, 512), F32) for i in range(2)]
            yT = T["yT%d" % l]
            P.dma('sp', bgT[:], T["b_gate"][l].rearrange("j (t p) -> p (j t)", p=128), w=["bgT"], slow=True)
            wg = T["w_gate"][l].rearrange("(k p) j d -> p k j d", p=128)
            wbr = T["w_branch"][l].rearrange("j (k p) d -> p j k d", p=128)
            for dmt in range(8):
                wi = dmt % 2
                for j in range(4):
                    P.dma('pool', wblk[wi][:, :, j, :], wg[:, :, j, dmt * 128:(dmt + 1) * 128], w=[("wblk", wi)])
                    P.dma('pool', wbb[wi][:, j, :, :], wbr[:, j, :, dmt * 128:(dmt + 1) * 128], w=[("wbb", wi)])
                for ch in range(8):
                    yi = ch % 2
                    for j in range(4):
                        P.dma('sp', ych[yi][:, j, :, :], yT[j, :, :, ch * 512:(ch + 1) * 512].rearrange("k p t -> p k t"),
                              r=[("yT", l)], w=[("ych", yi)])
                    ai = ch % 2
                    hkeys = [("hT", ch * 4 + q) for q in range(4)]
                    for j in range(4):
                        pg, pgk = psr.next()
                        for kk in range(8):
                            P.op('pe', lambda e, kk=kk, pg=pg, j=j, wi=wi, ch=ch: e.matmul(
                                psf[pg][:], lhsT=wblk[wi][:, kk, j, :], rhs=hT[:, kk, PAD + ch * 512:PAD + (ch + 1) * 512],
                                start=(kk == 0), stop=(kk == 7)), r=[("wblk", wi)] + hkeys, w=[pgk])
                        pb, pbk = psr.next()
                        for kk in range(2):
                            P.op('pe', lambda e, kk=kk, pb=pb, j=j, wi=wi, yi=yi: e.matmul(
                                psf[pb][:], lhsT=wbb[wi][:, j, kk, :], rhs=ych[yi][:, j, kk, :],
                                start=(kk == 0), stop=(kk == 1)), r=[("wbb", wi), ("ych", yi)], w=[pbk])
                        sj = j % 2
                        P.op('act', lambda e, pg=pg, j=j, sj=sj, dmt=dmt: e.activation(
                            out=sig[sj][:], in_=psf[pg][:], func=AF.Sigmoid, bias=bgT[:, j * 8 + dmt:j * 8 + dmt + 1]),
                            r=[pgk, "bgT"], w=[("sig", sj)])
                        if j == 0:
                            P.op('dve', lambda e, pb=pb, sj=sj, ai=ai: e.tensor_tensor(
                                out=acc[ai][:], in0=psf[pb][:], in1=sig[sj][:], op=ALU.mult),
                                r=[pbk, ("sig", sj)], w=[("acc", ai)])
                        else:
                            P.op('dve', lambda e, pb=pb, sj=sj: e.tensor_tensor(
                                out=tmp[sj][:], in0=psf[pb][:], in1=sig[sj][:], op=ALU.mult),
                                r=[pbk, ("sig", sj)], w=[("tmp", sj)])
                            if j < 3:
                                P.op('dve', lambda e, sj=sj, ai=ai: e.tensor_tensor(
                                    out=acc[ai][:], in0=acc[ai][:], in1=tmp[sj][:], op=ALU.add),
                                    r=[("tmp", sj)], w=[("acc", ai)])
                            else:
                                P.op('dve', lambda e, sj=sj, ai=ai, dmt=dmt, ch=ch: e.tensor_tensor(
                                    out=mixedT[:, dmt, ch * 512:(ch + 1) * 512], in0=acc[ai][:], in1=tmp[sj][:], op=ALU.add),
                                    r=[("tmp", sj), ("acc", ai)], w=[("mixedT", ch)])
            P.barrier()
            P.emit()

    def stage_outproj(l, mixedT):
        with ExitStack() as st:
            P.es = st
            nb = NT_bufs(st)
            xn = [P.sb(P.uniq("xn"), (128, D), F32) for i in range(2)]
            rxn = Rot(P.uniq("xnk"), 2)
            wo = P.sb(P.uniq("wo"), (128, 8, D), BF16)
            src = T["x"] if l == 0 else T["xres"]
            load_gain(T["norm2_g"][l])
            wod = T["w_out"][l].rearrange("(k p) d -> p k d", p=128)
            for kk in range(8):
                P.dma('pool', wo[:, kk, :], wod[:, kk, :], w=[("wo", kk)])
            for i in range(NT):
                j, xk = nb.rx.next()
                P.dma('sp', nb.xb[j][:], src[i * 128:(i + 1) * 128, :], r=[("xres", i)], w=[xk])
                jn, xnk = rxn.next()
                for half in range(2):
                    pg, pgk = psr.next()
                    for kk in range(8):
                        P.op('pe', lambda e, kk=kk, pg=pg, half=half, i=i: e.matmul(
                            psf[pg][:], lhsT=mixedT[:, kk, i * 128:(i + 1) * 128], rhs=wo[:, kk, half * 512:(half + 1) * 512],
                            start=(kk == 0), stop=(kk == 7)), r=[("mixedT", i // 4), ("wo", kk)], w=[pgk])
                    P.op('dve', lambda e, pg=pg, half=half, jn=jn, j=j: e.tensor_tensor(
                        out=xn[jn][:, half * 512:(half + 1) * 512], in0=psf[pg][:], in1=nb.xb[j][:, half * 512:(half + 1) * 512],
                        op=ALU.add), r=[pgk, xk], w=[xnk])
                P.dma('sp', T["xres"][i * 128:(i + 1) * 128, :], xn[jn][:], r=[xnk], w=[("xres", i)])
                norm_transpose(nb, xn[jn][:], xnk, i)
            P.barrier()
            P.emit()

    def stage_ffn(l, last):
        with ExitStack() as st:
            P.es = st
            G = 1024
            uT = P.sb(P.uniq("uT"), (128, NFF, G), BF16)
            w2s = P.sb(P.uniq("w2s"), (128, NFF, D), BF16)
            wblk = [P.sb(P.uniq("wblk"), (128, 8, 2, 128), BF16) for i in range(2)]
            xb = [P.sb(P.uniq("xb"), (128, D), F32) for i in range(2)]
            xn = [P.sb(P.uniq("xn"), (128, D), F32) for i in range(2)]
            sig = [P.sb(P.uniq("sig"), (128, 512), BF16) for i in range(2)]
            junk = P.sb(P.uniq("junk"), (128, D), BF16)
            rx = Rot(P.uniq("xbk"), 2)
            rxn = Rot(P.uniq("xnk"), 2)
            w1 = T["w_ff1"][l].rearrange("(k p) f -> p k f", p=128)
            w3 = T["w_ff3"][l].rearrange("(k p) f -> p k f", p=128)
            w2d = T["w_ff2"][l].rearrange("(k p) d -> p k d", p=128)
            for q in range(NFF):
                P.dma('pool', w2s[:, q, :], w2d[:, q, :], w=[("w2s", q)])
            if last:
                load_gain(T["final_g"])
            for g in range(S // G):
                for ft in range(NFF):
                    wi = ft % 2
                    P.dma('pool', wblk[wi][:, :, 0, :], w1[:, :, ft * 128:(ft + 1) * 128], w=[("wblk", wi)])
                    P.dma('pool', wblk[wi][:, :, 1, :], w3[:, :, ft * 128:(ft + 1) * 128], w=[("wblk", wi)])
                    for ch in range(G // 512):
                        t0 = g * G + ch * 512
                        hkeys = [("hT", t0 // 128 + q) for q in range(4)]
                        p1, p1k = psr.next()
                        for kk in range(8):
                            P.op('pe', lambda e, kk=kk, p1=p1, wi=wi, t0=t0: e.matmul(
                                psf[p1][:], lhsT=wblk[wi][:, kk, 0, :], rhs=hT[:, kk, PAD + t0:PAD + t0 + 512],
                                start=(kk == 0), stop=(kk == 7)), r=[("wblk", wi)] + hkeys, w=[p1k])
                        p3, p3k = psr.next()
                        for kk in range(8):
                            P.op('pe', lambda e, kk=kk, p3=p3, wi=wi, t0=t0: e.matmul(
                                psf[p3][:], lhsT=wblk[wi][:, kk, 1, :], rhs=hT[:, kk, PAD + t0:PAD + t0 + 512],
                                start=(kk == 0), stop=(kk == 7)), r=[("wblk", wi)] + hkeys, w=[p3k])
                        sj = ch % 2
                        P.op('act', lambda e, p1=p1, sj=sj: e.activation(out=sig[sj][:], in_=psf[p1][:], func=AF.Silu),
                             r=[p1k], w=[("sig", sj)])
                        P.op('dve', lambda e, p3=p3, sj=sj, ft=ft, ch=ch: e.tensor_tensor(
                            out=uT[:, ft, ch * 512:(ch + 1) * 512], in0=psf[p3][:], in1=sig[sj][:], op=ALU.mult),
                            r=[p3k, ("sig", sj)], w=[("uT", ch)])
                for tt in range(G // 128):
                    i = g * (G // 128) + tt
                    j, xk = rx.next()
                    P.dma('sp', xb[j][:], T["xres"][i * 128:(i + 1) * 128, :], r=[("xres", i)], w=[xk])
                    jn, xnk = rxn.next()
                    for half in range(2):
                        pg, pgk = psr.next()
                        for ft in range(NFF):
                            P.op('pe', lambda e, ft=ft, pg=pg, half=half, tt=tt: e.matmul(
                                psf[pg][:], lhsT=uT[:, ft, tt * 128:(tt + 1) * 128], rhs=w2s[:, ft, half * 512:(half + 1) * 512],
                                start=(ft == 0), stop=(ft == NFF - 1)), r=[("uT", tt // 4), ("w2s", ft)], w=[pgk])
                        P.op('dve', lambda e, pg=pg, half=half, jn=jn, j=j: e.tensor_tensor(
                            out=xn[jn][:, half * 512:(half + 1) * 512], in0=psf[pg][:], in1=xb[j][:, half * 512:(half + 1) * 512],
                            op=ALU.add), r=[pgk, xk], w=[xnk])
                    if not last:
                        P.dma('sp', T["xres"][i * 128:(i + 1) * 128, :], xn[jn][:], r=[xnk], w=[("xres", i)])
                    else:
                        xt = xn[jn][:]
                        P.op('act', lambda e, xt=xt: e.activation(out=junk[:], in_=xt, func=AF.Square, accum_out=stat[:, 4:5]),
                             r=[xnk], w=["junk", "stat2"])
                        P.op('act', lambda e: e.activation(out=stat[:, 5:6], in_=stat[:, 4:5], func=AF.Ln, bias=epsc[:, 0:1],
                                                           scale=1.0 / D), r=["stat2", "epsc"], w=["stat2"])
                        P.op('act', lambda e: e.activation(out=stat[:, 6:7], in_=stat[:, 5:6], func=AF.Exp, scale=-0.5),
                             r=["stat2"], w=["stat2"])
                        jo, ok = rx.next()
                        P.op('dve', lambda e, xt=xt, jo=jo: e.scalar_tensor_tensor(
                            out=xb[jo][:], in0=xt, scalar=stat[:, 6:7], in1=gb[:], op0=ALU.mult, op1=ALU.mult),
                            r=[xnk, "stat2", "gb"], w=[ok])
                        P.dma('sp', T["out"][i * 128:(i + 1) * 128, :], xb[jo][:], r=[ok], w=[("out", i)])
            P.barrier()
            P.emit()

    if not cfg.get("y_in"):
        attn_setup(k)
    for l in layers:
        stage_norm1(l)
        if not cfg.get("y_in"):
            mixers(k, l, cfg)
        if cfg.get("y_out"):
            continue
        with ExitStack() as st2:
            P.es = st2
            mixedT = P.sb(P.uniq("mixedT"), (128, 8, S), BF16)
            stage_merge(l, mixedT)
            stage_outproj(l, mixedT)
        stage_ffn(l, last=(l == layers[-1]))
    P.es = es
    P.barrier()
    P.emit()
    es.close()
    return nc


def load_cols(P, dst, src, key):
    P.dma('sp', dst, src.rearrange("n p -> p n"), w=[key], slow=True)


def hkeys_for(t0, n, lo=0, hi=0):
    a = max(0, (t0 - lo) // 128)
    b = min(NT - 1, (t0 + n - 1 + hi) // 128)
    return [("hT", q) for q in range(a, b + 1)]


def mixer_B(k, l):
    P, T, hT, psf, psr = k.P, k.T, k.hT, k.psf, k.psr
    with ExitStack() as st:
        P.es = st
        nm = P.uniq("B")
        wst = P.sb(nm + "wst", (128, 8, 256), F32)
        wj = P.sb(nm + "wj", (128, 8, 4, 256), BF16)
        wg = P.sb(nm + "wg", (128, 8, 256), BF16)
        cwb = P.sb(nm + "cwb", (128, 4 * 256), F32)
        pv = P.sb(nm + "pv", (128, 16), F32)
        sc = P.sb(nm + "sc", (128, 16), F32)
        onec = P.sb(nm + "one", (128, 1), F32)
        bd = P.sb(nm + "bd", (128, 8, 128), BF16)
        xc32 = P.sb(nm + "xc32", (128, S), F32)
        xcb = P.sb(nm + "xcb", (128, S), BF16)
        a_t = P.sb(nm + "a", (128, S), F32)
        u_t = P.sb(nm + "u", (128, S), F32)
        hs = P.sb(nm + "hs", (128, S), F32)
        gel = P.sb(nm + "gel", (128, S), BF16)
        t1 = [P.sb(nm + "t1%d" % i, (128, 512), F32) for i in range(2)]
        t2 = [P.sb(nm + "t2%d" % i, (128, 512), F32) for i in range(2)]
        t3 = [P.sb(nm + "t3%d" % i, (128, 512), F32) for i in range(2)]
        win = T["w_in"][l].rearrange("(k p) f -> p k f", p=128)
        for kk in range(8):
            P.dma('sp', wst[:, kk, :], win[:, kk, 1280:1536], w=[nm + "wst"])
            P.dma('pool', wg[:, kk, :], win[:, kk, 1536:1792], w=[nm + "wg"])
        P.dma('sp', cwb[:], T["lru_conv_w"][l].rearrange("j c -> (j c)").partition_broadcast(128), w=[nm + "cwb"])
        load_cols(P, pv[:, 0:2], T["lru_conv_b"][l].rearrange("(t p) -> t p", p=128), nm + "pv")
        load_cols(P, pv[:, 2:6], T["lru_ba"][l].rearrange("z (t p) -> (z t) p", p=128), nm + "pv")
        load_cols(P, pv[:, 6:10], T["lru_bx"][l].rearrange("z (t p) -> (z t) p", p=128), nm + "pv")
        load_cols(P, pv[:, 10:14], T["lru_lambda"][l].rearrange("z (t p) -> (z t) p", p=128), nm + "pv")
        P.op('dve', lambda e: e.memset(onec[:], 1.0), w=[nm + "one"])
        P.op('dve', lambda e: e.memset(bd[:], 0.0), w=[nm + "bd"])
        for z in range(2):
            for wh, wname in enumerate(("lru_wa", "lru_wx")):
                for n in range(4):
                    ct, hf = n // 2, n % 2
                    P.dma('pool', bd[hf * 64:(hf + 1) * 64, (z * 2 + wh) * 2 + ct, hf * 64:(hf + 1) * 64],
                          T[wname][l, z, n], w=[nm + "bd"])
        for j in range(4):
            for kk in range(8):
                P.op('pool', lambda e, j=j, kk=kk: e.tensor_tensor(out=wj[:, kk, j, :], in0=wst[:, kk, :],
                                                                     in1=cwb[:, j * 256:(j + 1) * 256], op=ALU.mult),
                     r=[nm + "wst", nm + "cwb"], w=[nm + "wj"])
        P.op('act', lambda e: e.activation(out=sc[:, 8:12], in_=pv[:, 10:14], func=AF.Exp, scale=-1.0),
             r=[nm + "pv"], w=[nm + "sc"])
        P.op('act', lambda e: e.activation(out=sc[:, 12:16], in_=sc[:, 8:12], func=AF.Ln, bias=onec[:, 0:1], scale=1.0),
             r=[nm + "sc", nm + "one"], w=[nm + "sc"])
        P.op('dve', lambda e: e.tensor_scalar(out=sc[:, 0:4], in0=sc[:, 12:16], scalar1=-8.0, scalar2=None, op0=ALU.mult),
             r=[nm + "sc"], w=[nm + "sc"])
        P.op('dve', lambda e: e.tensor_scalar(out=sc[:, 4:8], in0=sc[:, 12:16], scalar1=-16.0, scalar2=None, op0=ALU.mult),
             r=[nm + "sc"], w=[nm + "sc"])
        for ct in range(2):
            cs = slice(ct * 128, (ct + 1) * 128)
            for ch in range(8):
                t0 = ch * 512
                c0 = slice(t0, t0 + 512)
                pg, pgk = psr.next()
                hk = hkeys_for(t0, 512, 2, 1)
                for j in range(4):
                    for kk in range(8):
                        P.op('pe', lambda e, j=j, kk=kk, pg=pg, t0=t0, cs=cs: e.matmul(
                            psf[pg][:], lhsT=wj[:, kk, j, cs], rhs=hT[:, kk, PAD + t0 + j - 2:PAD + t0 + j - 2 + 512],
                            start=(j == 0 and kk == 0), stop=(j == 3 and kk == 7)), r=[nm + "wj"] + hk, w=[pgk])
                P.op('act', lambda e, pg=pg, c0=c0, ct=ct: e.activation(out=xc32[:, c0], in_=psf[pg][:], func=AF.Identity,
                                                                    bias=pv[:, ct:ct + 1]), r=[pgk, nm + "pv"], w=[nm + "xc32"])
                P.op('pool', lambda e, c0=c0: e.tensor_copy(out=xcb[:, c0], in_=xc32[:, c0]), r=[nm + "xc32"], w=[nm + "xcb"])
                pg2, pg2k = psr.next()
                for kk in range(8):
                    P.op('pe', lambda e, kk=kk, pg2=pg2, t0=t0, cs=cs: e.matmul(
                        psf[pg2][:], lhsT=wg[:, kk, cs], rhs=hT[:, kk, PAD + t0:PAD + t0 + 512],
                        start=(kk == 0), stop=(kk == 7)), r=[nm + "wg"] + hk, w=[pg2k])
                i2 = ch % 2
                P.op('act', lambda e, pg2=pg2, i2=i2: e.activation(out=t1[i2][:], in_=psf[pg2][:], func=AF.Identity),
                     r=[pg2k], w=[(nm + "t1", i2)])
                P.op('act', lambda e, pg2=pg2, i2=i2: e.activation(out=t2[i2][:], in_=psf[pg2][:], func=AF.Square),
                     r=[pg2k], w=[(nm + "t2", i2)])
                P.op('dve', lambda e, i2=i2: e.tensor_scalar(out=t2[i2][:], in0=t2[i2][:], scalar1=0.044715, scalar2=1.0,
                                                            op0=ALU.mult, op1=ALU.add), r=[], w=[(nm + "t2", i2)])
                P.op('dve', lambda e, i2=i2: e.tensor_tensor(out=t2[i2][:], in0=t2[i2][:], in1=t1[i2][:], op=ALU.mult),
                     r=[(nm + "t1", i2)], w=[(nm + "t2", i2)])
                P.op('act', lambda e, i2=i2: e.activation(out=t3[i2][:], in_=t2[i2][:], func=AF.Sigmoid,
                                                          scale=1.5957691216057308), r=[(nm + "t2", i2)], w=[(nm + "t3", i2)])
                P.op('dve', lambda e, i2=i2, c0=c0: e.tensor_tensor(out=gel[:, c0], in0=t3[i2][:], in1=t1[i2][:], op=ALU.mult),
                     r=[(nm + "t3", i2), (nm + "t1", i2)], w=[nm + "gel"])
            for z in range(2):
                zc = z * 2 + ct
                for ch in range(8):
                    t0 = ch * 512
                    c0 = slice(t0, t0 + 512)
                    i2 = ch % 2
                    pr, prk = psr.next()
                    P.op('pe', lambda e, pr=pr, c0=c0, z=z, ct=ct: e.matmul(psf[pr][:], lhsT=bd[:, (z * 2 + 0) * 2 + ct, :],
                                                                          rhs=xcb[:, c0], start=True, stop=True),
                         r=[nm + "bd", nm + "xcb"], w=[prk])
                    pi, pik = psr.next()
                    P.op('pe', lambda e, pi=pi, c0=c0, z=z, ct=ct: e.matmul(psf[pi][:], lhsT=bd[:, (z * 2 + 1) * 2 + ct, :],
                                                                          rhs=xcb[:, c0], start=True, stop=True),
                         r=[nm + "bd", nm + "xcb"], w=[pik])
                    P.op('act', lambda e, pr=pr, i2=i2, zc=zc: e.activation(out=t1[i2][:], in_=psf[pr][:], func=AF.Sigmoid,
                                                                        bias=pv[:, 2 + zc:3 + zc]), r=[prk, nm + "pv"], w=[(nm + "t1", i2)])
                    P.op('act', lambda e, pi=pi, i2=i2, zc=zc: e.activation(out=t2[i2][:], in_=psf[pi][:], func=AF.Sigmoid,
                                                                        bias=pv[:, 6 + zc:7 + zc]), r=[pik, nm + "pv"], w=[(nm + "t2", i2)])
                    P.op('act', lambda e, i2=i2, zc=zc, c0=c0: e.activation(out=a_t[:, c0], in_=t1[i2][:], func=AF.Exp,
                                                                        scale=sc[:, zc:zc + 1]), r=[(nm + "t1", i2), nm + "sc"], w=[nm + "a"])
                    P.op('act', lambda e, i2=i2, zc=zc: e.activation(out=t3[i2][:], in_=t1[i2][:], func=AF.Exp,
                                                                 scale=sc[:, 4 + zc:5 + zc]), r=[(nm + "t1", i2), nm + "sc"], w=[(nm + "t3", i2)])
                    P.op('act', lambda e, i2=i2: e.activation(out=t3[i2][:], in_=t3[i2][:], func=AF.Ln, bias=onec[:, 0:1], scale=-1.0),
                         r=[nm + "one"], w=[(nm + "t3", i2)])
                    P.op('act', lambda e, i2=i2: e.activation(out=t3[i2][:], in_=t3[i2][:], func=AF.Exp, scale=0.5),
                         r=[], w=[(nm + "t3", i2)])
                    P.op('dve', lambda e, i2=i2: e.tensor_tensor(out=t3[i2][:], in0=t3[i2][:], in1=t2[i2][:], op=ALU.mult),
                         r=[(nm + "t2", i2)], w=[(nm + "t3", i2)])
                    P.op('dve', lambda e, i2=i2, c0=c0: e.tensor_tensor(out=u_t[:, c0], in0=t3[i2][:], in1=xc32[:, c0], op=ALU.mult),
                         r=[(nm + "t3", i2), nm + "xc32"], w=[nm + "u"])
                if z == 0:
                    P.op('dve', lambda e: e.tensor_tensor_scan(out=hs[:, :], data0=a_t[:, :], data1=u_t[:, :], initial=0.0,
                                                               op0=ALU.mult, op1=ALU.add), r=[nm + "a", nm + "u"], w=[nm + "hs"])
                else:
                    P.op('dve', lambda e: e.tensor_tensor_scan(out=xc32[:, ::-1], data0=a_t[:, ::-1], data1=u_t[:, ::-1], initial=0.0,
                                                               op0=ALU.mult, op1=ALU.add), r=[nm + "a", nm + "u"], w=[nm + "xc32"])
            P.op('pool', lambda e: e.tensor_tensor(out=hs[:, :], in0=hs[:, :], in1=xc32[:, :], op=ALU.add),
                 r=[nm + "xc32"], w=[nm + "hs"])
            P.op('pool', lambda e: e.tensor_tensor(out=xcb[:, :], in0=hs[:, :], in1=gel[:, :], op=ALU.mult),
                 r=[nm + "hs", nm + "gel"], w=[nm + "xcb"])
            P.dma('sp', T["yT%d" % l][1, ct], xcb[:, :], r=[nm + "xcb"], w=[("yT", l)])
        P.barrier()
        P.emit()


def t5_bucket_np(rel):
    half = 16
    max_exact = 8
    n = np.abs(rel)
    large = max_exact + (np.log(np.maximum(n, 1) / max_exact) / math.log(1024 / max_exact) * (half - max_exact)).astype(np.int64)
    large = np.minimum(large, half - 1)
    return (rel > 0).astype(np.int64) * half + np.where(n < max_exact, n, large)


D_DILS = (1, 4, 16)


def attn_setup(k):
    P, T, psf, psr = k.P, k.T, k.psf, k.psr
    k.EB = P.sb("EB", (128, 3, 4, 3, 128), BF16)
    with ExitStack() as st:
        P.es = st
        rb = P.sb("rb_s", (32, 12), F32)
        rbh = P.sb("rbh_s", (32, 12), BF16)
        rbl = P.sb("rbl_s", (32, 12), BF16)
        oh = P.sb("oh_s", (32, 3, 512), BF16)
        bnd = P.sb("bnd_s", (4, 3, 512), F32)
        vm = P.sb("vm_s", (4, 512), F32)
        P.dma('sp', rb[:], T["rel_bias"], w=["rb_s"])
        P.dma('pool', oh[:], T["c_oh"], w=["oh_s"])
        P.op('dve', lambda e: e.tensor_copy(out=rbh[:], in_=rb[:]), r=["rb_s"], w=["rbh_s"])
        P.op('dve', lambda e: e.tensor_tensor(out=rbl[:], in0=rb[:], in1=rbh[:], op=ALU.subtract), r=["rb_s", "rbh_s"], w=["rbl_s"])
        P.dma('sp', vm[:], T["c_valid"].partition_broadcast(4), w=["vm_s"])
        for g in range(3):
            pg, pgk = psr.next()
            P.op('pe', lambda e, g=g, pg=pg: e.matmul(psf[pg][0:4, :], lhsT=rbh[:, 4 * g:4 * g + 4], rhs=oh[:, g, :],
                                                      start=True, stop=False), r=["rbh_s", "oh_s"], w=[pgk])
            P.op('pe', lambda e, g=g, pg=pg: e.matmul(psf[pg][0:4, :], lhsT=rbl[:, 4 * g:4 * g + 4], rhs=oh[:, g, :],
                                                      start=False, stop=True), r=["rbl_s", "oh_s"], w=[pgk])
            P.op('act', lambda e, g=g, pg=pg: e.activation(out=bnd[:, g, :], in_=psf[pg][0:4, :], func=AF.Exp),
                 r=[pgk], w=["bnd_s"])
            P.op('dve', lambda e, g=g: e.tensor_tensor(out=bnd[:, g, :], in0=bnd[:, g, :], in1=vm[:, :], op=ALU.mult),
                 r=["vm_s"], w=["bnd_s"])
        P.dma('sp', T["bandD"].rearrange("g h j -> h g j"), bnd[:], r=["bnd_s"], w=["bandD"])
        bt = T["bandD"]
        hk = P.sb("hk_s", (128, 3, 4, 3, 128), F32)
        for g in range(3):
            for h in range(4):
                for pi, pos in enumerate((-1, 0, 1)):
                    off = (g * 4 + h) * 512 + 129 + 128 * pos
                    src = bass.AP(tensor=bt.tensor, offset=off, ap=[[1, 128], [1, 128]])
                    P.dma('sp', hk[:, g, h, pi, :], src, r=["bandD"], w=["hk_s"])
                    P.op('dve', lambda e, g=g, h=h, pi=pi: e.tensor_copy(out=k.EB[:, g, h, pi, :], in_=hk[:, g, h, pi, ::-1]),
                         r=["hk_s"], w=["EB"])
        P.barrier()
        P.emit()


def mixer_D(k, l):
    P, T, hT, psf, psr, psb, ident = k.P, k.T, k.hT, k.psf, k.psr, k.psb, k.ident
    win = T["w_in"][l].rearrange("(k p) f -> p k f", p=128)
    for g, dil in enumerate(D_DILS):
        n = S // dil
        ntl = n // 128
        with ExitStack() as st:
            P.es = st
            nm = P.uniq("D")
            wq = P.sb(nm + "wq", (128, 8, 256), BF16)
            wk = P.sb(nm + "wk", (128, 8, 256), BF16)
            wv = P.sb(nm + "wv", (128, 8, 256), BF16)
            QT = P.sb(nm + "QT", (128, 2, S), BF16)
            KT = P.sb(nm + "KT", (128, 2, S), BF16)
            Vt = P.sb(nm + "Vt", (128, 32, 4, 65), BF16)
            Eb = [P.sb(nm + "E%d" % i, (128, 3, 128), BF16) for i in range(2)]
            Pm = [P.sb(nm + "Pm%d" % i, (128, 3, 128), BF16) for i in range(2)]
            ob = [P.sb(nm + "ob%d" % i, (128, 260), F32) for i in range(2)]
            for kk in range(8):
                P.dma('pool', wq[:, kk, :], win[:, kk, 2560 + g * 256:2560 + (g + 1) * 256], w=[nm + "wq"])
                P.dma('pool', wk[:, kk, :], win[:, kk, 3328 + g * 256:3328 + (g + 1) * 256], w=[nm + "wk"])
                P.dma('pool', wv[:, kk, :], win[:, kk, 4096 + g * 256:4096 + (g + 1) * 256], w=[nm + "wv"])
            P.op('dve', lambda e: e.memset(Vt[:, :, :, 64:65], 1.0), w=[nm + "Vt"])
            for (wsb, dst, wkey, dkey) in ((wq, QT, nm + "wq", nm + "QT"), (wk, KT, nm + "wk", nm + "KT")):
                for pr in range(2):
                    for ch in range(8):
                        t0 = ch * 512
                        pg, pgk = psr.next()
                        for kk in range(8):
                            P.op('pe', lambda e, kk=kk, pg=pg, t0=t0, pr=pr, wsb=wsb: e.matmul(
                                psf[pg][:], lhsT=wsb[:, kk, pr * 128:(pr + 1) * 128], rhs=hT[:, kk, PAD + t0:PAD + t0 + 512],
                                start=(kk == 0), stop=(kk == 7)), r=[wkey] + hkeys_for(t0, 512), w=[pgk])
                        if dil == 1:
                            P.op('act', lambda e, pg=pg, t0=t0, pr=pr, dst=dst: e.copy(out=dst[:, pr, t0:t0 + 512], in_=psf[pg][:]),
                                 r=[pgk], w=[dkey])
                        else:
                            m = 512 // dil
                            P.op('act', lambda e, pg=pg, t0=t0, pr=pr, dst=dst, m=m: e.copy(
                                out=dst[:, pr, :].rearrange("p (r i) -> p r i", r=dil)[:, :, t0 // dil:t0 // dil + m],
                                in_=psf[pg][:].rearrange("p (i r) -> p r i", r=dil)), r=[pgk], w=[dkey])
            for r_ in range(dil):
                for j in range(ntl):
                    ti = r_ * ntl + j
                    c_lo = PAD + r_ + dil * 128 * j
                    pg, pgk = psr.next()
                    for kk in range(8):
                        P.op('pe', lambda e, kk=kk, pg=pg, c_lo=c_lo: e.matmul(
                            psf[pg][:, 0:256], lhsT=hT[:, kk, c_lo:c_lo + dil * 127 + 1:dil], rhs=wv[:, kk, :],
                            start=(kk == 0), stop=(kk == 7)), r=[nm + "wv"] + hkeys_for(c_lo - PAD, dil * 127 + 1), w=[pgk])
                    P.op('dve', lambda e, pg=pg, ti=ti: e.tensor_copy(
                        out=Vt[:, ti, :, 0:64], in_=psf[pg][:, 0:256].rearrange("p (h d) -> p h d", h=4)), r=[pgk], w=[nm + "Vt"])
            cnt = 0
            for r_ in range(dil):
                for i in range(ntl):
                    qc = r_ * n + i * 128
                    po, pok = psr.next()
                    for h in range(4):
                        pr, hp = h // 2, h % 2
                        prt = slice(hp * 64, (hp + 1) * 64)
                        poss = [pp for pp in (-1, 0, 1) if 0 <= i + pp < ntl]
                        lo, hi = poss[0] + 1, poss[-1] + 2
                        ps_, psk = psr.next()
                        for pp in poss:
                            kc = r_ * n + (i + pp) * 128
                            P.op('pe', lambda e, ps_=ps_, pp=pp, kc=kc, qc=qc, pr=pr, prt=prt: e.matmul(
                                psf[ps_][:, (pp + 1) * 128:(pp + 2) * 128], lhsT=KT[prt, pr, kc:kc + 128], rhs=QT[prt, pr, qc:qc + 128],
                                start=True, stop=True), r=[nm + "KT", nm + "QT"], w=[psk])
                        ei = cnt % 2
                        cnt += 1
                        P.op('act', lambda e, ps_=ps_, ei=ei, lo=lo, hi=hi: e.activation(
                            out=Eb[ei][:, lo:hi, :], in_=psf[ps_][:, lo * 128:hi * 128].rearrange("p (a q) -> p a q", a=hi - lo),
                            func=AF.Exp, scale=0.125), r=[psk], w=[(nm + "E", ei)])
                        P.op('dve', lambda e, ei=ei, lo=lo, hi=hi, h=h: e.tensor_tensor(
                            out=Pm[ei][:, lo:hi, :], in0=Eb[ei][:, lo:hi, :], in1=k.EB[:, g, h, lo:hi, :], op=ALU.mult),
                            r=[(nm + "E", ei), "EB"], w=[(nm + "Pm", ei)])
                        for pp in poss:
                            ti = r_ * ntl + i + pp
                            P.op('pe', lambda e, po=po, pp=pp, ti=ti, ei=ei, h=h, poss=poss: e.matmul(
                                psf[po][:, h * 65:(h + 1) * 65], lhsT=Pm[ei][:, pp + 1, :], rhs=Vt[:, ti, h, :],
                                start=(pp == poss[0]), stop=(pp == poss[-1])), r=[(nm + "Pm", ei), nm + "Vt"], w=[pok])
                    oi = (r_ * ntl + i) % 2
                    P.op('act', lambda e, po=po, oi=oi: e.copy(out=ob[oi][:], in_=psf[po][:, 0:260]), r=[pok], w=[(nm + "ob", oi)])
                    row0 = r_ + dil * 128 * i
                    dst = T["Og"][g, row0:row0 + dil * 127 + 1:dil, :]
                    P.dma('sp', dst, ob[oi][:], r=[(nm + "ob", oi)], w=[("Og", g)])
            P.barrier()
            P.emit()
    with ExitStack() as st:
        P.es = st
        nm = P.uniq("Dc")
        og = [[P.sb(nm + "og%d_%d" % (i, g), (128, 260), F32) for g in range(3)] for i in range(2)]
        rd = [P.sb(nm + "rd%d" % i, (128, 4), F32) for i in range(2)]
        yd = [P.sb(nm + "yd%d" % i, (128, 256), BF16) for i in range(2)]
        ydT = [P.sb(nm + "ydT%d" % i, (128, 2, 512), BF16) for i in range(2)]
        for i in range(NT):
            bi = i % 2
            for g in range(3):
                P.dma('sp', og[bi][g][:], T["Og"][g, i * 128:(i + 1) * 128, :], r=[("Og", g)], w=[(nm + "og", bi, g)])
            P.op('pool', lambda e, bi=bi: e.tensor_tensor(out=og[bi][0][:], in0=og[bi][0][:], in1=og[bi][1][:], op=ALU.add),
                 r=[(nm + "og", bi, 1)], w=[(nm + "og", bi, 0)])
            P.op('pool', lambda e, bi=bi: e.tensor_tensor(out=og[bi][0][:], in0=og[bi][0][:], in1=og[bi][2][:], op=ALU.add),
                 r=[(nm + "og", bi, 2)], w=[(nm + "og", bi, 0)])
            P.op('dve', lambda e, bi=bi: e.reciprocal(out=rd[bi][:], in_=og[bi][0][:].rearrange("p (h d) -> p h d", h=4)[:, :, 64]),
                 r=[(nm + "og", bi, 0)], w=[(nm + "rd", bi)])
            for h in range(4):
                P.op('dve', lambda e, bi=bi, h=h: e.tensor_scalar(out=yd[bi][:, h * 64:(h + 1) * 64], in0=og[bi][0][:, h * 65:h * 65 + 64],
                                                                scalar1=rd[bi][:, h:h + 1], scalar2=None, op0=ALU.mult),
                     r=[(nm + "og", bi, 0), (nm + "rd", bi)], w=[(nm + "yd", bi)])
            for c in range(2):
                P.op('pe', lambda e, c=c, bi=bi: e.transpose(out=psb[:, c, :], in_=yd[bi][:, c * 128:(c + 1) * 128], identity=ident[:]),
                     r=[(nm + "yd", bi), "ident"], w=["psb"])
            ti = (i // 4) % 2
            q4 = i % 4
            P.op('act', lambda e, ti=ti, q4=q4: e.copy(out=ydT[ti][:, :, q4 * 128:(q4 + 1) * 128], in_=psb[:, 0:2, :]),
                 r=["psb"], w=[(nm + "ydT", ti)])
            if q4 == 3:
                c0 = (i // 4) * 512
                for c in range(2):
                    P.dma('sp', T["yT%d" % l][3, c, :, c0:c0 + 512], ydT[ti][:, c, :], r=[(nm + "ydT", ti)], w=[("yT", l)])
        P.barrier()
        P.emit()


def mixer_A(k, l):
    P, T, hT, psf, psr, psb, ident = k.P, k.T, k.hT, k.psf, k.psr, k.psb, k.ident
    win = T["w_in"][l].rearrange("(k p) f -> p k f", p=128)
    with ExitStack() as st:
        P.es = st
        nm = P.uniq("A")
        wA = P.sb(nm + "wA", (128, 8, 1280), BF16)
        Vt = P.sb(nm + "Vt", (128, NT, 256), BF16)
        sgl = P.sb(nm + "sgl", (128, NT, 256), BF16)
        oacc = P.sb(nm + "oacc", (128, NT, 256), F32)
        lbb = P.sb(nm + "lbb", (128, 256), F32)
        oml = P.sb(nm + "oml", (128, 256), F32)
        gnb = P.sb(nm + "gnb", (128, 256), F32)
        lg = P.sb(nm + "lg", (128, 2, 256), F32)
        Mq = P.sb(nm + "Mq", (128, 2, 128), BF16)
        sel = P.sb(nm + "sel", (128, 2, 6), BF16)
        cm = P.sb(nm + "cm", (128, 2, 512), F32)
        S32 = P.sb(nm + "S32", (128, 2, 64), F32)
        Sb = P.sb(nm + "Sb", (128, 2, 64), BF16)
        stp = P.sb(nm + "stp", (128, 2, 64), F32)
        NB = 2
        f1 = [P.sb(nm + "f1_%d" % i, (128, 256), F32) for i in range(NB)]
        f2 = [P.sb(nm + "f2_%d" % i, (128, 256), F32) for i in range(NB)]
        f3 = [P.sb(nm + "f3_%d" % i, (128, 256), F32) for i in range(NB)]
        kkb = [P.sb(nm + "kk_%d" % i, (128, 256), F32) for i in range(NB)]
        lfh = [P.sb(nm + "lfh_%d" % i, (128, 256), BF16) for i in range(NB)]
        lfl = [P.sb(nm + "lfl_%d" % i, (128, 256), BF16) for i in range(NB)]
        q32 = [P.sb(nm + "q32_%d" % i, (128, 256), F32) for i in range(NB)]
        qt = [P.sb(nm + "qt_%d" % i, (128, 256), BF16) for i in range(NB)]
        kt = [P.sb(nm + "kt_%d" % i, (128, 256), BF16) for i in range(NB)]
        qkT = [P.sb(nm + "qkT_%d" % i, (128, 4, 128), BF16) for i in range(NB)]
        sc6 = [P.sb(nm + "sc6_%d" % i, (128, 12), F32) for i in range(NB)]
        qh = [P.sb(nm + "qh_%d" % i, (128, 2, 128), BF16) for i in range(NB)]
        kh = [P.sb(nm + "kh_%d" % i, (128, 2, 128), BF16) for i in range(NB)]
        Am = [P.sb(nm + "Am_%d" % i, (128, 512), BF16) for i in range(NB)]
        ot = [P.sb(nm + "ot_%d" % i, (128, 256), F32) for i in range(NB)]
        ss = [P.sb(nm + "ss_%d" % i, (128, 8), F32) for i in range(NB)]
        ya = [P.sb(nm + "ya_%d" % i, (128, 256), BF16) for i in range(NB)]
        yaT = [P.sb(nm + "yaT_%d" % i, (128, 2, 512), BF16) for i in range(2)]
        onec = P.sb(nm + "one", (128, 1), F32)
        epsc = k.epsc
        for kk in range(8):
            P.dma('pool', wA[:, kk, :], win[:, kk, 0:1280], w=[nm + "wA"])
        P.dma('pool', Mq[:], T["c_Mq"], w=[nm + "Mq"])
        P.dma('pool', sel[:], T["c_sel"], w=[nm + "sel"])
        P.dma('sp', cm[:], T["c_cm"], w=[nm + "cm"])
        P.dma('sp', gnb[:], T["hgrn_norm_g"][l].partition_broadcast(128), w=[nm + "gnb"])
        P.op('dve', lambda e: e.memset(onec[:], 1.0), w=[nm + "one"])
        if l == 0:
            P.op('dve', lambda e: e.memset(lbb[:], 0.0), w=[nm + "lbb"])
        else:
            P.dma('sp', lg[:, 0, :], T["hgrn_lb_logits"][0].partition_broadcast(128), w=[nm + "lg"])
            P.dma('sp', lg[:, 1, :], T["hgrn_lb_logits"][1].partition_broadcast(128), w=[nm + "lg"])
            P.op('dve', lambda e: e.tensor_tensor(out=lbb[:], in0=lg[:, 0, :], in1=lg[:, 1, :], op=ALU.subtract),
                 r=[nm + "lg"], w=[nm + "lbb"])
            P.op('act', lambda e: e.activation(out=lbb[:], in_=lbb[:], func=AF.Exp), w=[nm + "lbb"])
            P.op('dve', lambda e: e.tensor_scalar(out=lbb[:], in0=lbb[:], scalar1=1.0, scalar2=None, op0=ALU.add), w=[nm + "lbb"])
            P.op('dve', lambda e: e.reciprocal(out=lbb[:], in_=lbb[:]), w=[nm + "lbb"])
        P.op('dve', lambda e: e.tensor_scalar(out=oml[:], in0=lbb[:], scalar1=-1.0, scalar2=1.0, op0=ALU.mult, op1=ALU.add),
             r=[nm + "lbb"], w=[nm + "oml"])

        def hk(i):
            return [("hT", i)]

        for i in range(NT):
            b = i % NB
            pg, pgk = psr.next()
            for kk in range(8):
                P.op('pe', lambda e, kk=kk, pg=pg, i=i: e.matmul(
                    psf[pg][:], lhsT=hT[:, kk, PAD + i * 128:PAD + (i + 1) * 128], rhs=wA[:, kk, 768:1280],
                    start=(kk == 0), stop=(kk == 7)), r=[nm + "wA"] + hk(i), w=[pgk])
            P.op('act', lambda e, pg=pg, i=i: e.copy(out=Vt[:, i, :], in_=psf[pg][:, 0:256]), r=[pgk], w=[(nm + "Vt", i)])
            P.op('act', lambda e, pg=pg, b=b: e.activation(out=f1[b][:], in_=psf[pg][:, 256:512], func=AF.Exp, scale=-1.0),
                 r=[pgk], w=[(nm + "f1", b)])
            P.op('pool', lambda e, b=b: e.tensor_scalar(out=f1[b][:], in0=f1[b][:], scalar1=1.0, scalar2=None, op0=ALU.add),
                 w=[(nm + "f1", b)])
            P.op('dve', lambda e, b=b: e.reciprocal(out=f1[b][:], in_=f1[b][:]), w=[(nm + "f1", b)])
            P.op('dve', lambda e, pg=pg, b=b: e.tensor_tensor(out=f2[b][:], in0=psf[pg][:, 256:512], in1=f1[b][:], op=ALU.mult),
                 r=[pgk, (nm + "f1", b)], w=[(nm + "f2", b)])
            P.op('pool', lambda e, b=b, i=i: e.tensor_tensor(out=sgl[:, i, :], in0=f2[b][:], in1=gnb[:], op=ALU.mult),
                 r=[(nm + "f2", b), nm + "gnb"], w=[(nm + "sgl", i)])

        cnt = [0]

        def prep(z, i):
            b = cnt[0] % NB
            cnt[0] += 1
            pg, pgk = psr.next()
            f0 = 256 if z == 0 else 512
            for kk in range(8):
                P.op('pe', lambda e, kk=kk, pg=pg: e.matmul(
                    psf[pg][:, 0:256], lhsT=hT[:, kk, PAD + i * 128:PAD + (i + 1) * 128], rhs=wA[:, kk, 0:256],
                    start=(kk == 0), stop=(kk == 7)), r=[nm + "wA"] + hk(i), w=[pgk])
            for kk in range(8):
                P.op('pe', lambda e, kk=kk, pg=pg: e.matmul(
                    psf[pg][:, 256:512], lhsT=hT[:, kk, PAD + i * 128:PAD + (i + 1) * 128], rhs=wA[:, kk, f0:f0 + 256],
                    start=(kk == 0), stop=(kk == 7)), r=[nm + "wA"] + hk(i), w=[pgk])
            P.op('act', lambda e: e.copy(out=q32[b][:], in_=psf[pg][:, 0:256]), r=[pgk], w=[(nm + "q32", b)])
            P.op('act', lambda e: e.activation(out=f1[b][:], in_=psf[pg][:, 256:512], func=AF.Exp, scale=-1.0),
                 r=[pgk], w=[(nm + "f1", b)])
            P.op('dve', lambda e: e.tensor_scalar(out=f1[b][:], in0=f1[b][:], scalar1=1.0, scalar2=None, op0=ALU.add),
                 w=[(nm + "f1", b)])
            P.op('dve', lambda e: e.reciprocal(out=f1[b][:], in_=f1[b][:]), w=[(nm + "f1", b)])
            P.op('dve', lambda e: e.tensor_tensor(out=f2[b][:], in0=f1[b][:], in1=oml[:], op=ALU.mult),
                 r=[(nm + "f1", b), nm + "oml"], w=[(nm + "f2", b)])
            P.op('dve', lambda e: e.tensor_tensor(out=kkb[b][:], in0=oml[:], in1=f2[b][:], op=ALU.subtract),
                 r=[(nm + "f2", b), nm + "oml"], w=[(nm + "kk", b)])
            P.op('dve', lambda e: e.tensor_tensor(out=f3[b][:], in0=f2[b][:], in1=lbb[:], op=ALU.add),
                 r=[(nm + "f2", b), nm + "lbb"], w=[(nm + "f3", b)])
            P.op('act', lambda e: e.activation(out=f3[b][:], in_=f3[b][:], func=AF.Ln), w=[(nm + "f3", b)])
            P.op('dve', lambda e: e.tensor_copy(out=lfh[b][:], in_=f3[b][:]), r=[(nm + "f3", b)], w=[(nm + "lfh", b)])
            P.op('dve', lambda e: e.tensor_tensor(out=lfl[b][:], in0=f3[b][:], in1=lfh[b][:], op=ALU.subtract),
                 r=[(nm + "f3", b), (nm + "lfh", b)], w=[(nm + "lfl", b)])
            pm, pmk = psr.next()
            for hl, src_ in enumerate((lfh, lfl)):
                P.op('pe', lambda e, hl=hl, src_=src_: e.matmul(psf[pm][:, 0:256], lhsT=Mq[:, z, :], rhs=src_[b][:],
                                                              start=(hl == 0), stop=(hl == 1)),
                     r=[nm + "Mq", (nm + "lfh", b), (nm + "lfl", b)], w=[pmk])
            for pr in range(2):
                for hl, src_ in enumerate((lfh, lfl)):
                    P.op('pe', lambda e, pr=pr, hl=hl, src_=src_: e.matmul(
                        psf[pm][:, 256 + 8 * pr:256 + 8 * pr + 6], lhsT=src_[b][:, pr * 128:(pr + 1) * 128],
                        rhs=sel[:, z, :], start=(hl == 0), stop=(hl == 1)),
                        r=[nm + "sel", (nm + "lfh", b), (nm + "lfl", b)], w=[pmk])
            P.op('act', lambda e: e.activation(out=f1[b][:], in_=psf[pm][:, 0:256], func=AF.Exp), r=[pmk], w=[(nm + "f1", b)])
            P.op('act', lambda e: e.activation(out=f2[b][:], in_=psf[pm][:, 0:256], func=AF.Exp, scale=-1.0), r=[pmk], w=[(nm + "f2", b)])
            P.op('act', lambda e: e.activation(out=sc6[b][:].rearrange("p (a c) -> p a c", a=2),
                                               in_=psf[pm][:, 256:272].rearrange("p (a c) -> p a c", a=2)[:, :, 0:6], func=AF.Exp),
                 r=[pmk], w=[(nm + "sc6", b)])
            P.op('dve', lambda e: e.tensor_tensor(out=qt[b][:], in0=q32[b][:], in1=f1[b][:], op=ALU.mult),
                 r=[(nm + "q32", b), (nm + "f1", b)], w=[(nm + "qt", b)])
            P.op('dve', lambda e: e.tensor_tensor(out=kt[b][:], in0=kkb[b][:], in1=f2[b][:], op=ALU.mult),
                 r=[(nm + "kk", b), (nm + "f2", b)], w=[(nm + "kt", b)])
            for c in range(2):
                P.op('pe', lambda e, c=c: e.transpose(out=psb[:, c, :], in_=qt[b][:, c * 128:(c + 1) * 128], identity=ident[:]),
                     r=[(nm + "qt", b), "ident"], w=["psb"])
            for c in range(2):
                P.op('pe', lambda e, c=c: e.transpose(out=psb[:, 2 + c, :], in_=kt[b][:, c * 128:(c + 1) * 128], identity=ident[:]),
                     r=[(nm + "kt", b), "ident"], w=["psb"])
            P.op('act', lambda e: e.copy(out=qkT[b][:], in_=psb[:, 0:4, :]), r=["psb"], w=[(nm + "qkT", b)])
            far = slice(0, 64) if z == 0 else slice(64, 128)
            near = slice(64, 128) if z == 0 else slice(0, 64)
            for pr in range(2):
                P.op('dve', lambda e, pr=pr: e.tensor_scalar(out=qh[b][:, pr, 0:64], in0=psb[:, pr, 0:64],
                                                            scalar1=sc6[b][:, 6 * pr:6 * pr + 1], scalar2=None, op0=ALU.mult),
                     r=["psb", (nm + "sc6", b)], w=[(nm + "qh", b)])
                P.op('dve', lambda e, pr=pr: e.tensor_scalar(out=qh[b][:, pr, 64:128], in0=psb[:, pr, 64:128],
                                                            scalar1=sc6[b][:, 6 * pr + 1:6 * pr + 2], scalar2=None, op0=ALU.mult),
                     r=["psb", (nm + "sc6", b)], w=[(nm + "qh", b)])
                P.op('dve', lambda e, pr=pr: e.tensor_scalar(out=kh[b][:, pr, far], in0=psb[:, 2 + pr, far],
                                                            scalar1=sc6[b][:, 6 * pr + 3:6 * pr + 4], scalar2=None, op0=ALU.mult),
                     r=["psb", (nm + "sc6", b)], w=[(nm + "kh", b)])
                P.op('act', lambda e, pr=pr: e.copy(out=kh[b][:, pr, near], in_=psb[:, 2 + pr, near]),
                     r=["psb"], w=[(nm + "kh", b)])
            return b

        def step(z, i, b):
            P.op('dve', lambda e: e.tensor_copy(out=Sb[:], in_=S32[:]), r=[nm + "S32"], w=[nm + "Sb"])
            ps_, psk = psr.next()
            tA = slice(0, 64)
            tB = slice(64, 128)
            for h in range(4):
                pr, hp = h // 2, h % 2
                prt = slice(hp * 64, (hp + 1) * 64)
                kA = qkT[b][prt, 2 + pr, :] if z == 0 else kh[b][prt, pr, :]
                kB = kh[b][prt, pr, :] if z == 0 else qkT[b][prt, 2 + pr, :]
                P.op('pe', lambda e, h=h, pr=pr, prt=prt, kA=kA: e.matmul(psf[ps_][:, h * 128:h * 128 + 64], lhsT=kA,
                                                                       rhs=qkT[b][prt, pr, tA], start=True, stop=True),
                     r=[(nm + "qkT", b), (nm + "kh", b)], w=[psk], serial=True)
                P.op('pe', lambda e, h=h, pr=pr, prt=prt, kB=kB: e.matmul(psf[ps_][:, h * 128 + 64:h * 128 + 128], lhsT=kB,
                                                                       rhs=qkT[b][prt, pr, tB], start=True, stop=True),
                     r=[(nm + "qkT", b), (nm + "kh", b)], w=[psk], serial=True)
            P.op('dve', lambda e: e.tensor_tensor(out=Am[b][:], in0=psf[ps_][:], in1=cm[:, z, :], op=ALU.mult),
                 r=[psk, nm + "cm"], w=[(nm + "Am", b)])
            po, pok = psr.next()
            for h in range(4):
                pr, hp = h // 2, h % 2
                prt = slice(hp * 64, (hp + 1) * 64)
                P.op('pe', lambda e, h=h: e.matmul(psf[po][:, h * 64:(h + 1) * 64], lhsT=Am[b][:, h * 128:(h + 1) * 128],
                                                   rhs=Vt[:, i, h * 64:(h + 1) * 64], start=True, stop=False),
                     r=[(nm + "Am", b), (nm + "Vt", i)], w=[pok])
                P.op('pe', lambda e, h=h, pr=pr, prt=prt: e.matmul(psf[po][:, h * 64:(h + 1) * 64], lhsT=qh[b][prt, pr, :],
                                                              rhs=Sb[prt, pr, :], start=False, stop=True),
                     r=[(nm + "qh", b), nm + "Sb"], w=[pok], serial=True)
            pG, pGk = psr.next()
            for pr in range(2):
                for blk in range(2):
                    bp = slice(blk * 64, (blk + 1) * 64)
                    P.op('pe', lambda e, pr=pr, blk=blk, bp=bp: e.matmul(
                        psf[pG][:, pr * 256 + blk * 128:pr * 256 + (blk + 1) * 128], lhsT=kt[b][bp, pr * 128:(pr + 1) * 128],
                        rhs=Vt[bp, i, pr * 128:(pr + 1) * 128], start=True, stop=True),
                        r=[(nm + "kt", b), (nm + "Vt", i)], w=[pGk], serial=True)
            for pr in range(2):
                P.op('dve', lambda e, pr=pr: e.tensor_scalar(out=stp[:, pr, :], in0=S32[:, pr, :], scalar1=sc6[b][:, 6 * pr + 2:6 * pr + 3],
                                                            scalar2=None, op0=ALU.mult), r=[nm + "S32", (nm + "sc6", b)], w=[nm + "stp"])
                for hp in range(2):
                    prt = slice(hp * 64, (hp + 1) * 64)
                    c0 = pr * 256 + hp * 64
                    P.op('dve', lambda e, pr=pr, prt=prt, c0=c0: e.scalar_tensor_tensor(
                        out=stp[prt, pr, :], in0=psf[pG][prt, c0:c0 + 64],
                        scalar=sc6[b][prt, 6 * pr + 4:6 * pr + 5], in1=stp[prt, pr, :], op0=ALU.mult, op1=ALU.add),
                        r=[pGk, (nm + "sc6", b)], w=[nm + "stp"])
                    P.op('dve', lambda e, pr=pr, prt=prt, c0=c0: e.scalar_tensor_tensor(
                        out=S32[prt, pr, :], in0=psf[pG][prt, c0 + 128:c0 + 192],
                        scalar=sc6[b][prt, 6 * pr + 5:6 * pr + 6], in1=stp[prt, pr, :], op0=ALU.mult, op1=ALU.add),
                        r=[pGk, (nm + "sc6", b), nm + "stp"], w=[nm + "S32"])
            return po, pok

        amode = k.dbg.get("A_mode", "full")
        for z in range(2):
            if amode == "prepass":
                break
            P.op('dve', lambda e: e.memset(S32[:], 0.0), w=[nm + "S32"])
            order = list(range(NT)) if z == 0 else list(range(NT - 1, -1, -1))
            if amode in ("prep", "step"):
                order = order[:4]
            if amode == "prep1":
                order = order[:1]
                if z == 1:
                    break
            if amode == "prepr":
                order = order[:4]
                if z == 1:
                    break
            if amode == "prepz":
                order = order[:1]
            for n_, i in enumerate(order):
                b = prep(z, i)
                if amode in ("prep", "prep1", "prepr", "prepz"):
                    continue
                po, pok = step(z, i, b)
                if z == 0:
                    P.op('act', lambda e, po=po, i=i: e.copy(out=oacc[:, i, :], in_=psf[po][:, 0:256]), r=[pok], w=[(nm + "oacc", i)])
                else:
                    P.op('dve', lambda e, po=po, i=i, b=b: e.tensor_tensor(out=ot[b][:], in0=psf[po][:, 0:256], in1=oacc[:, i, :], op=ALU.add),
                         r=[pok, (nm + "oacc", i)], w=[(nm + "ot", b)])
                    P.op('pool', lambda e, b=b: e.tensor_tensor(out=f1[b][:], in0=ot[b][:], in1=ot[b][:], op=ALU.mult),
                         r=[(nm + "ot", b)], w=[(nm + "f1", b)])
                    P.op('dve', lambda e, b=b: e.tensor_reduce(out=ss[b][:, 0:4], in_=f1[b][:].rearrange("p (h d) -> p h d", h=4),
                                                               axis=mybir.AxisListType.X, op=ALU.add),
                         r=[(nm + "f1", b)], w=[(nm + "ss", b)])
                    P.op('act', lambda e, b=b: e.activation(out=ss[b][:, 4:8], in_=ss[b][:, 0:4], func=AF.Ln, bias=epsc[:, 0:1],
                                                            scale=1.0 / 64), r=["epsc"], w=[(nm + "ss", b)])
                    P.op('act', lambda e, b=b: e.activation(out=ss[b][:, 0:4], in_=ss[b][:, 4:8], func=AF.Exp, scale=-0.5),
                         w=[(nm + "ss", b)])
                    for h in range(4):
                        P.op('dve', lambda e, b=b, h=h, i=i: e.scalar_tensor_tensor(
                            out=ya[b][:, h * 64:(h + 1) * 64], in0=ot[b][:, h * 64:(h + 1) * 64], scalar=ss[b][:, h:h + 1],
                            in1=sgl[:, i, h * 64:(h + 1) * 64], op0=ALU.mult, op1=ALU.mult),
                            r=[(nm + "ot", b), (nm + "ss", b), (nm + "sgl", i)], w=[(nm + "ya", b)])
                    for c in range(2):
                        P.op('pe', lambda e, c=c, b=b: e.transpose(out=psb[:, 4 + c, :], in_=ya[b][:, c * 128:(c + 1) * 128], identity=ident[:]),
                             r=[(nm + "ya", b), "ident"], w=["psb"])
                    grp = i // 4
                    ti = grp % 2
                    q4 = i % 4
                    P.op('act', lambda e, ti=ti, q4=q4: e.copy(out=yaT[ti][:, :, q4 * 128:(q4 + 1) * 128], in_=psb[:, 4:6, :]),
                         r=["psb"], w=[(nm + "yaT", ti)])
                    if q4 == 0:
                        c0 = grp * 512
                        for c in range(2):
                            P.dma('sp', T["yT%d" % l][0, c, :, c0:c0 + 512], yaT[ti][:, c, :], r=[(nm + "yaT", ti)], w=[("yT", l)])
        P.barrier()
        P.emit()


NF = 33
TWO_PI = 2.0 * math.pi


def mixer_C(k, l):
    P, T, hT, psf, psb, ident = k.P, k.T, k.hT, k.psf, k.psb, k.ident
    win = T["w_in"][l].rearrange("(k p) f -> p k f", p=128)
    psr5 = Rot("psf", 5)
    nm = P.uniq("C")
    Bc, Bs = T["c_dftc"], T["c_dfts"]
    Hd = T["Hd"]

    with ExitStack() as st:
        P.es = st
        comb = P.sb(nm + "comb", (128, NT, 1024), BF16)
        with ExitStack() as st2:
            P.es = st2
            zh = P.sb(nm + "zh", (33, S), BF16)
            zl = P.sb(nm + "zl", (33, S), BF16)
            w1h = P.sb(nm + "w1h", (33, 64), BF16)
            w1l = P.sb(nm + "w1l", (33, 64), BF16)
            w2h = P.sb(nm + "w2h", (64, 64), BF16)
            w2l = P.sb(nm + "w2l", (64, 64), BF16)
            hsh = [P.sb(nm + "hsh0", (64, 512), BF16)] * 2
            hsl = [P.sb(nm + "hsl0", (64, 512), BF16)] * 2
            hd1 = P.sb(nm + "hd1", (64, S), F32)
            hd2 = P.sb(nm + "hd2", (64, S), BF16)
            w1 = P.sb(nm + "w1", (33, 64), F32)
            w2 = P.sb(nm + "w2", (64, 64), F32)
            w3 = P.sb(nm + "w3", (64, 1024), BF16)
            pc = P.sb(nm + "pc", (64, 8), F32)
            arg = [P.sb(nm + "arg%d" % i, (64, 512), F32) for i in range(2)]
            wrp = [P.sb(nm + "wrp0", (64, 512), F32)] * 2
            dec = [P.sb(nm + "dec0", (128, 256), F32)] * 2
            hfd = [P.sb(nm + "hfd%d" % i, (128, 1024), F32) for i in range(2)]
            sq = [P.sb(nm + "sq0", (128, 1024), BF16)] * 2
            rsb = P.sb(nm + "rsb", (128, 1024), F32)
            onesb = P.sb(nm + "onesb", (128, 128), BF16)
            P.dma('sp', w1[:], T["hy_w1"][l], w=[nm + "w1"])
            P.dma('sp', w2[:], T["hy_w2"][l], w=[nm + "w2"])
            P.dma('pool', w3[:], T["hy_w3"][l], w=[nm + "w3"])
            load_cols(P, pc[:, 0:1], T["hy_b1"][l].rearrange("(o p) -> o p", o=1), nm + "pc")
            load_cols(P, pc[:, 1:2], T["hy_freq"][l].rearrange("(o p) -> o p", o=1), nm + "pc")
            load_cols(P, pc[:, 2:3], T["hy_b2"][l].rearrange("(o p) -> o p", o=1), nm + "pc")
            P.op('dve', lambda e: e.tensor_tensor(out=pc[:, 3:4], in0=pc[:, 0:1], in1=pc[:, 1:2], op=ALU.mult), w=[nm + "pc"])
            P.op('dve', lambda e: e.tensor_tensor(out=pc[:, 4:5], in0=pc[:, 2:3], in1=pc[:, 1:2], op=ALU.mult), w=[nm + "pc"])
            P.op('dve', lambda e: e.memset(onesb[:], 1.0), w=[nm + "onesb"])
            P.dma('pool', zh[:], T["c_zTh"], w=[nm + "zh"])
            P.dma('pool', zl[:], T["c_zTl"], w=[nm + "zl"])
            for (wf32, wh, wl, key) in ((w1, w1h, w1l, nm + "w1"), (w2, w2h, w2l, nm + "w2")):
                P.op('dve', lambda e, wf32=wf32, wh=wh: e.tensor_copy(out=wh[:], in_=wf32[:]), r=[key], w=[key + "h"])
                P.op('dve', lambda e, wf32=wf32, wh=wh, wl=wl: e.tensor_tensor(out=wl[:], in0=wf32[:], in1=wh[:], op=ALU.subtract),
                     r=[key, key + "h"], w=[key + "l"])
            for lay in range(2):
                kdim = 33 if lay == 0 else 64
                wh, wl = (w1h, w1l) if lay == 0 else (w2h, w2l)
                wkey = nm + ("w1" if lay == 0 else "w2")
                bcol = 3 if lay == 0 else 4
                dst = hd1 if lay == 0 else hd2
                dkey = nm + ("hd1" if lay == 0 else "hd2")
                for ch in range(8):
                    c0 = slice(ch * 512, (ch + 1) * 512)
                    ai = ch % 2
                    if lay == 0:
                        sh, sl, skeys = zh[0:33, c0], zl[0:33, c0], [nm + "zh", nm + "zl"]
                    else:
                        P.op('dve', lambda e, ai=ai, c0=c0: e.tensor_copy(out=hsh[ai][:], in_=hd1[:, c0]), r=[nm + "hd1"], w=[nm + "hsh"])
                        P.op('dve', lambda e, ai=ai, c0=c0: e.tensor_tensor(out=hsl[ai][:], in0=hd1[:, c0], in1=hsh[ai][:], op=ALU.subtract),
                             r=[nm + "hd1", nm + "hsh"], w=[nm + "hsl"])
                        sh, sl, skeys = hsh[ai][:], hsl[ai][:], [nm + "hsh", nm + "hsl"]
                    pg, pgk = psr5.next()
                    terms = ((wh, sh), (wh, sl), (wl, sh))
                    for ti_, (wa_, sa_) in enumerate(terms):
                        P.op('pe', lambda e, pg=pg, wa_=wa_, sa_=sa_, ti_=ti_, kdim=kdim: e.matmul(
                            psf[pg][0:64, :], lhsT=wa_[0:kdim, :], rhs=sa_, start=(ti_ == 0), stop=(ti_ == 2)),
                            r=[wkey + "h", wkey + "l"] + skeys, w=[pgk])
                    P.op('dve', lambda e, pg=pg, ai=ai, bcol=bcol: e.tensor_scalar(
                        out=arg[ai][:], in0=psf[pg][0:64, :], scalar1=pc[:, 1:2], scalar2=pc[:, bcol:bcol + 1],
                        op0=ALU.mult, op1=ALU.add), r=[pgk, nm + "pc"], w=[(nm + "arg", ai)])
                    for _ in range(2):
                        for (cmp_, bnd, per) in ((ALU.is_gt, math.pi, -TWO_PI), (ALU.is_lt, -math.pi, TWO_PI)):
                            P.op('dve', lambda e, ai=ai, cmp_=cmp_, bnd=bnd, per=per: e.tensor_scalar(
                                out=wrp[ai][:], in0=arg[ai][:], scalar1=bnd, scalar2=per, op0=cmp_, op1=ALU.mult),
                                r=[(nm + "arg", ai)], w=[nm + "wrp"])
                            P.op('dve', lambda e, ai=ai: e.tensor_tensor(out=arg[ai][:], in0=arg[ai][:], in1=wrp[ai][:], op=ALU.add),
                                 r=[nm + "wrp"], w=[(nm + "arg", ai)])
                    P.op('act', lambda e, ai=ai, dst=dst, c0=c0: e.activation(out=dst[:, c0], in_=arg[ai][:], func=AF.Sin),
                         r=[(nm + "arg", ai)], w=[dkey])

            def filt_tile(i, bi):
                P.dma('sp', dec[bi][:], T["c_decay"][i * 128:(i + 1) * 128, :], w=[nm + "dec"])
                for half in range(2):
                    pg, pgk = psr5.next()
                    P.op('pe', lambda e, pg=pg, half=half: e.matmul(
                        psf[pg][:], lhsT=hd2[0:64, i * 128:(i + 1) * 128], rhs=w3[0:64, half * 512:(half + 1) * 512],
                        start=True, stop=True), r=[nm + "hd2", nm + "w3"], w=[pgk])
                    for q in range(2):
                        P.op('dve', lambda e, pg=pg, half=half, q=q: e.tensor_tensor(
                            out=hfd[bi][:, half * 512 + q * 256:half * 512 + (q + 1) * 256], in0=psf[pg][:, q * 256:(q + 1) * 256],
                            in1=dec[bi][:], op=ALU.mult), r=[pgk, nm + "dec"], w=[(nm + "hfd", bi)])

            for i in range(NT):
                bi = i % 2
                filt_tile(i, bi)
                P.op('act', lambda e, bi=bi: e.activation(out=sq[bi][:], in_=hfd[bi][:], func=AF.Square),
                     r=[(nm + "hfd", bi)], w=[nm + "sq"])
                for half in range(2):
                    P.op('pe', lambda e, half=half, bi=bi, i=i: e.matmul(
                        psf[5 + half][:], lhsT=onesb[:], rhs=sq[bi][:, half * 512:(half + 1) * 512],
                        start=(i == 0), stop=(i == NT - 1)), r=[nm + "onesb", nm + "sq"], w=[("psf", 5 + half)])
            for half in range(2):
                P.op('act', lambda e, half=half: e.activation(out=rsb[:, half * 512:(half + 1) * 512], in_=psf[5 + half][:],
                                                              func=AF.Ln, bias=k.epsc[:, 0:1]), r=[("psf", 5 + half), "epsc"], w=[nm + "rsb"])
            P.op('act', lambda e: e.activation(out=rsb[:], in_=rsb[:], func=AF.Exp, scale=-0.5), w=[nm + "rsb"])
            for i in range(NT):
                bi = i % 2
                filt_tile(i, bi)
                P.op('pool', lambda e, bi=bi: e.tensor_tensor(out=hfd[bi][:], in0=hfd[bi][:], in1=rsb[:], op=ALU.mult),
                     r=[nm + "rsb"], w=[(nm + "hfd", bi)])
                for o in range(2):
                    fw = slice(o * 512, o * 512 + 256)
                    bw = slice(o * 512 + 256, o * 512 + 512)
                    P.op('dve', lambda e, bi=bi, i=i, o=o, fw=fw, bw=bw: e.tensor_tensor(
                        out=comb[:, i, o * 256:(o + 1) * 256], in0=hfd[bi][:, fw], in1=hfd[bi][:, bw], op=ALU.add),
                        r=[(nm + "hfd", bi)], w=[(nm + "comb", i)])
                    P.op('pool', lambda e, bi=bi, i=i, o=o, fw=fw, bw=bw: e.tensor_tensor(
                        out=comb[:, i, 512 + o * 256:512 + (o + 1) * 256], in0=hfd[bi][:, fw], in1=hfd[bi][:, bw], op=ALU.subtract),
                        r=[(nm + "hfd", bi)], w=[(nm + "comb", i)])
            P.barrier()
            P.emit()
        with ExitStack() as st2:
            P.es = st2
            bcb = [P.sb(nm + "bcb%d" % i, (128, NF, 128), BF16) for i in range(2)]
            bsb = [P.sb(nm + "bsb%d" % i, (128, NF, 128), BF16) for i in range(2)]
            hb_b = P.sb(nm + "hbb", (128, 512), F32)
            wfc = P.sb(nm + "wfc", (128, 2 * NF), F32)
            hst = [P.sb(nm + "hst%d" % i, (128, 2, 2, 256), F32) for i in range(2)]
            P.dma('sp', hb_b[:], T["hy_bias"][l].rearrange("o c -> (o c)").partition_broadcast(128), w=[nm + "hbb"])
            load_cols(P, wfc[:, 0:NF], T["c_wf"].rearrange("(t p) -> t p", p=128), nm + "wfc")
            P.op('dve', lambda e: e.tensor_scalar(out=wfc[:, NF:2 * NF], in0=wfc[:, 0:NF], scalar1=-1.0, scalar2=None, op0=ALU.mult),
                 r=[nm + "wfc"], w=[nm + "wfc2"])
            for ft in range(NF):
                bi = ft % 2
                P.dma('sp', bcb[bi][:], Bc[ft], w=[(nm + "bcb", bi)])
                P.dma('sp', bsb[bi][:], Bs[ft], w=[(nm + "bsb", bi)])
                pre, prek = psr5.next()
                pim, pimk = psr5.next()
                for tt in range(NT):
                    P.op('pe', lambda e, tt=tt, bi=bi, pre=pre: e.matmul(psf[pre][:], lhsT=bcb[bi][:, tt, :], rhs=comb[:, tt, 0:512],
                                                                      start=(tt == 0), stop=(tt == NT - 1)),
                         r=[(nm + "bcb", bi), (nm + "comb", tt)], w=[prek])
                for tt in range(NT):
                    P.op('pe', lambda e, tt=tt, bi=bi, pim=pim: e.matmul(psf[pim][:], lhsT=bsb[bi][:, tt, :], rhs=comb[:, tt, 512:1024],
                                                                      start=(tt == 0), stop=(tt == NT - 1)),
                         r=[(nm + "bsb", bi), (nm + "comb", tt)], w=[pimk])
                P.op('dve', lambda e, pre=pre, bi=bi: e.tensor_tensor(out=hst[bi][:, :, 0, :], in0=psf[pre][:].rearrange("p (o c) -> p o c", o=2),
                                                                    in1=hb_b[:].rearrange("p (o c) -> p o c", o=2), op=ALU.add),
                     r=[prek, nm + "hbb"], w=[(nm + "hst", bi)])
                P.op('pool', lambda e, bi=bi, ft=ft: e.tensor_scalar(out=hst[bi][:, :, 0, :], in0=hst[bi][:, :, 0, :], scalar1=wfc[:, ft:ft + 1],
                                                                   scalar2=None, op0=ALU.mult), r=[nm + "wfc"], w=[(nm + "hst", bi)])
                P.op('dve', lambda e, pim=pim, bi=bi, ft=ft: e.tensor_scalar(out=hst[bi][:, :, 1, :], in0=psf[pim][:].rearrange("p (o c) -> p o c", o=2),
                                                                          scalar1=wfc[:, NF + ft:NF + ft + 1], scalar2=None, op0=ALU.mult),
                     r=[pimk, nm + "wfc2"], w=[(nm + "hst", bi)])
                P.dma('sp', Hd[ft], hst[bi][:], r=[(nm + "hst", bi)], w=[("Hd", ft)])
            P.barrier()
            P.emit()

    with ExitStack() as st:
        P.es = st
        u1 = P.sb(nm + "u1", (128, NT, 256), BF16)
        x1s = P.sb(nm + "x1s", (128, NT, 256), BF16)
        x2s = P.sb(nm + "x2s", (128, NT, 256), BF16)
        with ExitStack() as st2:
            P.es = st2
            wj = P.sb(nm + "wj", (128, 8, 3, 768), BF16)
            wst = P.sb(nm + "wst", (128, 8, 768), F32)
            cwb = P.sb(nm + "cwb", (128, 3 * 768), F32)
            cbb = P.sb(nm + "cbb", (128, 768), F32)
            for kk in range(8):
                P.dma('sp', wst[:, kk, :], win[:, kk, 1792:2560], w=[nm + "wst"])
            P.dma('sp', cwb[:], T["hy_conv_w"][l].rearrange("j c -> (j c)").partition_broadcast(128), w=[nm + "cwb"])
            P.dma('sp', cbb[:], T["hy_conv_b"][l].partition_broadcast(128), w=[nm + "cbb"])
            for j in range(3):
                for kk in range(8):
                    P.op('pool' if kk % 2 else 'dve', lambda e, j=j, kk=kk: e.tensor_tensor(
                        out=wj[:, kk, j, :], in0=wst[:, kk, :], in1=cwb[:, j * 768:(j + 1) * 768], op=ALU.mult),
                        r=[nm + "wst", nm + "cwb"], w=[nm + "wj"])
            for i in range(NT):
                for (f0, fn) in ((0, 512), (512, 256)):
                    pg, pgk = psr5.next()
                    for j in range(3):
                        for kk in range(8):
                            c_lo = PAD + i * 128 + j - 1
                            P.op('pe', lambda e, j=j, kk=kk, pg=pg, c_lo=c_lo, f0=f0, fn=fn: e.matmul(
                                psf[pg][:, 0:fn], lhsT=hT[:, kk, c_lo:c_lo + 128], rhs=wj[:, kk, j, f0:f0 + fn],
                                start=(j == 0 and kk == 0), stop=(j == 2 and kk == 7)),
                                r=[nm + "wj"] + hkeys_for(i * 128, 128, 1, 1), w=[pgk])
                    if f0 == 0:
                        P.op('dve', lambda e, pg=pg, i=i: e.tensor_tensor(out=u1[:, i, :], in0=psf[pg][:, 0:256], in1=cbb[:, 0:256], op=ALU.add),
                             r=[pgk, nm + "cbb"], w=[(nm + "u1", i)])
                        P.op('dve', lambda e, pg=pg, i=i: e.tensor_tensor(out=x1s[:, i, :], in0=psf[pg][:, 256:512], in1=cbb[:, 256:512], op=ALU.add),
                             r=[pgk, nm + "cbb"], w=[(nm + "x1s", i)])
                    else:
                        P.op('dve', lambda e, pg=pg, i=i: e.tensor_tensor(out=x2s[:, i, :], in0=psf[pg][:, 0:256], in1=cbb[:, 512:768], op=ALU.add),
                             r=[pgk, nm + "cbb"], w=[(nm + "x2s", i)])
            P.barrier()
            P.emit()
        with ExitStack() as st2:
            P.es = st2
            Y = P.sb(nm + "Y", (128, NF, 2, 256), BF16)
            bcb = [P.sb(nm + "bcbx%d" % i, (128, NF, 128), BF16) for i in range(2)]
            bsb = [P.sb(nm + "bsbx%d" % i, (128, NF, 128), BF16) for i in range(2)]
            hl = [P.sb(nm + "hl%d" % i, (128, 2, 256), F32) for i in range(2)]
            ure = [P.sb(nm + "ure%d" % i, (128, 256), F32) for i in range(2)]
            tA = [P.sb(nm + "tA%d" % i, (128, 256), F32) for i in range(2)]
            tB = [P.sb(nm + "tB%d" % i, (128, 256), F32) for i in range(2)]
            yc = [P.sb(nm + "yc%d" % i, (128, 256), BF16) for i in range(2)]
            ycT = [P.sb(nm + "ycT%d" % i, (128, 2, 512), BF16) for i in range(2)]
            for o in range(2):
                for ft in range(NF):
                    bi = ft % 2
                    P.dma('sp', bcb[bi][:], Bc[ft], w=[(nm + "bcb", bi)])
                    P.dma('sp', bsb[bi][:], Bs[ft], w=[(nm + "bsb", bi)])
                    P.dma('sp', hl[bi][:], Hd[ft, :, o, :, :], r=[("Hd", ft)], w=[(nm + "hl", bi)])
                    pre, prek = psr5.next()
                    for tt in range(NT):
                        P.op('pe', lambda e, tt=tt, bi=bi, pre=pre: e.matmul(psf[pre][:, 0:256], lhsT=bcb[bi][:, tt, :], rhs=u1[:, tt, :],
                                                                          start=(tt == 0), stop=(tt == NT - 1)),
                             r=[(nm + "bcb", bi), (nm + "u1", tt)], w=[prek])
                    for tt in range(NT):
                        P.op('pe', lambda e, tt=tt, bi=bi, pre=pre: e.matmul(psf[pre][:, 256:512], lhsT=bsb[bi][:, tt, :], rhs=u1[:, tt, :],
                                                                          start=(tt == 0), stop=(tt == NT - 1)),
                             r=[(nm + "bsb", bi), (nm + "u1", tt)], w=[prek])
                    P.op('act', lambda e, pre=pre, bi=bi: e.copy(out=ure[bi][:], in_=psf[pre][:, 0:256]), r=[prek], w=[(nm + "ure", bi)])
                    P.op('dve', lambda e, bi=bi: e.tensor_tensor(out=tA[bi][:], in0=ure[bi][:], in1=hl[bi][:, 0, :], op=ALU.mult),
                         r=[(nm + "ure", bi), (nm + "hl", bi)], w=[(nm + "tA", bi)])
                    P.op('dve', lambda e, pre=pre, bi=bi: e.tensor_tensor(out=tB[bi][:], in0=psf[pre][:, 256:512], in1=hl[bi][:, 1, :], op=ALU.mult),
                         r=[prek, (nm + "hl", bi)], w=[(nm + "tB", bi)])
                    P.op('pool', lambda e, bi=bi, ft=ft: e.tensor_tensor(out=Y[:, ft, 0, :], in0=tA[bi][:], in1=tB[bi][:], op=ALU.add),
                         r=[(nm + "tA", bi), (nm + "tB", bi)], w=[(nm + "Y", ft)])
                    P.op('dve', lambda e, pre=pre, bi=bi: e.tensor_tensor(out=tB[bi][:], in0=psf[pre][:, 256:512], in1=hl[bi][:, 0, :], op=ALU.mult),
                         r=[prek, (nm + "hl", bi)], w=[(nm + "tB", bi)])
                    P.op('pool', lambda e, bi=bi: e.tensor_tensor(out=tA[bi][:], in0=ure[bi][:], in1=hl[bi][:, 1, :], op=ALU.mult),
                         r=[(nm + "ure", bi), (nm + "hl", bi)], w=[(nm + "tA", bi)])
                    P.op('pool', lambda e, bi=bi, ft=ft: e.tensor_tensor(out=Y[:, ft, 1, :], in0=tB[bi][:], in1=tA[bi][:], op=ALU.subtract),
                         r=[(nm + "tA", bi), (nm + "tB", bi)], w=[(nm + "Y", ft)])
                for tt in range(NT):
                    bi = tt % 2
                    P.dma('sp', bcb[bi][:], Bc[tt], w=[(nm + "bcb", bi)])
                    P.dma('sp', bsb[bi][:], Bs[tt], w=[(nm + "bsb", bi)])
                    py, pyk = psr5.next()
                    for ft in range(NF):
                        P.op('pe', lambda e, ft=ft, bi=bi, py=py: e.matmul(psf[py][:, 0:256], lhsT=bcb[bi][:, ft, :], rhs=Y[:, ft, 0, :],
                                                                        start=(ft == 0), stop=False),
                             r=[(nm + "bcb", bi), (nm + "Y", ft)], w=[pyk])
                    for ft in range(NF):
                        P.op('pe', lambda e, ft=ft, bi=bi, py=py: e.matmul(psf[py][:, 0:256], lhsT=bsb[bi][:, ft, :], rhs=Y[:, ft, 1, :],
                                                                        start=False, stop=(ft == NF - 1)),
                             r=[(nm + "bsb", bi), (nm + "Y", ft)], w=[pyk])
                    if o == 0:
                        P.op('dve', lambda e, py=py, tt=tt: e.tensor_tensor(out=u1[:, tt, :], in0=psf[py][:, 0:256], in1=x1s[:, tt, :], op=ALU.mult),
                             r=[pyk, (nm + "x1s", tt)], w=[(nm + "u1", tt)])
                    else:
                        P.op('dve', lambda e, py=py, tt=tt, bi=bi: e.tensor_tensor(out=yc[bi][:], in0=psf[py][:, 0:256], in1=x2s[:, tt, :], op=ALU.mult),
                             r=[pyk, (nm + "x2s", tt)], w=[(nm + "yc", bi)])
                        for c in range(2):
                            P.op('pe', lambda e, c=c, bi=bi: e.transpose(out=psb[:, c, :], in_=yc[bi][:, c * 128:(c + 1) * 128], identity=ident[:]),
                                 r=[(nm + "yc", bi), "ident"], w=["psb"])
                        ti = (tt // 4) % 2
                        q4 = tt % 4
                        P.op('act', lambda e, ti=ti, q4=q4: e.copy(out=ycT[ti][:, :, q4 * 128:(q4 + 1) * 128], in_=psb[:, 0:2, :]),
                             r=["psb"], w=[(nm + "ycT", ti)])
                        if q4 == 3:
                            c0 = (tt // 4) * 512
                            for c in range(2):
                                P.dma('sp', T["yT%d" % l][2, c, :, c0:c0 + 512], ycT[ti][:, c, :], r=[(nm + "ycT", ti)], w=[("yT", l)])
            P.barrier()
            P.emit()


def mixers(k, l, cfg):
    only = cfg.get("only", "ABCD")
    if "A" in only:
        mixer_A(k, l)
    if "B" in only:
        mixer_B(k, l)
    if "C" in only:
        mixer_C(k, l)
    if "D" in only:
        mixer_D(k, l)


def make_consts():
    c = {}
    c["c_ident"] = np.eye(128, dtype=np.float32).astype(ml_dtypes.bfloat16)
    oh = np.zeros((32, 3, 512), np.float32)
    valid = np.zeros((512,), np.float32)
    for g, dil in enumerate(D_DILS):
        for j in range(512):
            rel = j - 256
            if abs(rel) <= 64:
                b = int(t5_bucket_np(np.array([rel * dil]))[0])
                oh[b, g, j] = 1.0
                valid[j] = 1.0
    c["c_oh"] = oh
    c["c_valid"] = valid
    si = np.arange(128)[:, None]
    ti = np.arange(128)[None, :]
    Mq = np.zeros((128, 2, 128), np.float32)
    reff = np.where(ti < 64, 31, 95)
    refb = np.where(ti < 64, 32, 96)
    Mq[:, 0, :] = (si <= ti).astype(np.float32) - (si <= reff).astype(np.float32)
    Mq[:, 1, :] = (si >= ti).astype(np.float32) - (si >= refb).astype(np.float32)
    c["c_Mq"] = Mq
    sv = np.arange(128)
    sel = np.zeros((128, 2, 6), np.float32)
    sel[:, 0, 0] = sv <= 31
    sel[:, 0, 1] = sv <= 95
    sel[:, 0, 2] = 1.0
    sel[:, 0, 3] = (sv > 31) & (sv <= 95)
    sel[:, 0, 4] = sv > 31
    sel[:, 0, 5] = sv > 95
    sel[:, 1, 0] = sv >= 32
    sel[:, 1, 1] = sv >= 96
    sel[:, 1, 2] = 1.0
    sel[:, 1, 3] = (sv >= 32) & (sv < 96)
    sel[:, 1, 4] = sv < 32
    sel[:, 1, 5] = sv < 96
    c["c_sel"] = sel
    cm = np.zeros((128, 2, 512), np.float32)
    cm[:, 0, :] = np.tile((si <= ti).astype(np.float32), (1, 4))
    cm[:, 1, :] = np.tile((si >= ti).astype(np.float32), (1, 4))
    c["c_cm"] = cm
    L = S
    t = np.linspace(0.0, 1.0, L, dtype=np.float32)[:, None]
    bands = 16
    w = (2.0 * math.pi * np.arange(L, dtype=np.float32)[:, None] / L).astype(np.float32)
    fr = np.linspace(1e-4, bands - 1, bands, dtype=np.float32)[None]
    z = np.concatenate([t, np.cos(fr * w), -np.sin(fr * w)], axis=-1).astype(np.float32)
    zT = np.ascontiguousarray(z.T)
    zh_ = zT.astype(ml_dtypes.bfloat16)
    c["c_zTh"] = zh_
    c["c_zTl"] = (zT - zh_.astype(np.float32)).astype(ml_dtypes.bfloat16)
    deltas = np.linspace(math.log(1e-2) / 1.5, math.log(1e-2) / 0.3, 256, dtype=np.float32)
    c["c_decay"] = np.exp(-t * np.abs(deltas)[None, :]).astype(np.float32)
    n = 2 * L
    idx = np.arange(NF * 128, dtype=np.int64)
    prod = (idx[:, None] * idx[None, :]) % n
    ang = prod.astype(np.float64) * (2.0 * math.pi / n)
    ok = ((idx[:, None] <= L) & (idx[None, :] <= L))
    for nme, fn in (("c_dftc", np.cos), ("c_dfts", np.sin)):
        m = np.where(ok, fn(ang), 0.0).astype(np.float32)
        m = m.reshape(NF, 128, NF, 128).transpose(2, 1, 0, 3)
        c[nme] = np.ascontiguousarray(m).astype(ml_dtypes.bfloat16)
    wf = np.zeros((NF * 128,), np.float32)
    wf[0:L + 1] = 2.0 / n
    wf[0] = 1.0 / n
    wf[L] = 1.0 / n
    c["c_wf"] = wf
    return c


CONST_SPECS = {"c_ident": ((128, 128), BF16), "c_oh": ((32, 3, 512), F32), "c_valid": ((512,), F32), "c_Mq": ((128, 2, 128), F32), "c_sel": ((128, 2, 6), F32), "c_cm": ((128, 2, 512), F32), "c_zTh": ((33, S), BF16), "c_zTl": ((33, S), BF16), "c_decay": ((S, 256), F32), "c_dftc": ((33, 128, 33, 128), BF16), "c_dfts": ((33, 128, 33, 128), BF16), "c_wf": ((33 * 128,), F32)}


_CONST = {}


def consts():
    if not _CONST:
        _CONST.update(make_consts())
    return _CONST


def kernel(**inputs):
    cfg = {}
    nc = build(cfg)
    x = np.ascontiguousarray(inputs["x"], dtype=np.float32)
    shared = {name: np.ascontiguousarray(inputs[name], dtype=np.float32) for name, _ in WEIGHT_SPECS}
    shared.update(consts())
    in_maps = []
    for c in range(8):
        m = dict(shared)
        m["x"] = x[c]
        in_maps.append(m)
    res = run_bass_kernel_spmd(nc, in_maps, core_ids=list(range(8)))
    return np.stack([np.asarray(r["out"], dtype=np.float32) for r in res.results], axis=0)
```
